# Optimizing a Trainium2 kernel written in Bass

```python
import math
import jax, jax.numpy as jnp
from jax import lax
import numpy as np

D_MODEL = 1024
BATCH = 32
SEQ = 2048
DEPTH = 4

GRID_W = 64
CTX_LEN = 256
HEAD_DIM = 64
ROPE_THETA = 10000.0
EPS = 1e-6
Q_BLOCK = 128
A_Q_HEADS = 8
A_KV_HEADS = 2
A_WIDTH = A_Q_HEADS * HEAD_DIM
A_KV_WIDTH = A_KV_HEADS * HEAD_DIM
B_WIDTH = D_MODEL // 2
B_WINDOWS = (2, 4, 8, 16)
B_GROUPS = len(B_WINDOWS)
B_GROUP_W = B_WIDTH // B_GROUPS
EVEN_IN = A_WIDTH + 2 * A_KV_WIDTH + B_WIDTH
EVEN_MIX = A_WIDTH + B_WIDTH
C_Q_HEADS = 16
C_KV_HEADS = 4
C_WIDTH = C_Q_HEADS * HEAD_DIM
C_KV_WIDTH = C_KV_HEADS * HEAD_DIM
ODD_IN = C_WIDTH + 2 * C_KV_WIDTH
WINDOW = 128
N_SIDE = WINDOW // Q_BLOCK
BAND_BLOCKS = 2 * N_SIDE + 1
BAND_LEN = BAND_BLOCKS * Q_BLOCK
FFN_HIDDEN = int(math.ceil(8 * D_MODEL / 3 / 256)) * 256
N_EVEN = (DEPTH + 1) // 2
N_ODD = DEPTH // 2

kernel_name = "hybrid_dit_gqa_pool_swa_prefix"


def rmsnorm(x, g):
    xf = x.astype(jnp.float32)
    y = xf * lax.rsqrt(jnp.mean(xf * xf, axis=-1, keepdims=True) + EPS)
    return (y * g.astype(jnp.float32)).astype(x.dtype)


def modulate(x, g, shift, scale):
    return rmsnorm(x, g) * (1 + scale) + shift


def split_heads(t, h):
    return t.reshape(t.shape[0], t.shape[1], h, HEAD_DIM)


def axial_rope(x, rows, cols):
    hd = x.shape[-1]
    quarter = hd // 4
    freqs = ROPE_THETA ** (-jnp.arange(quarter, dtype=jnp.float32) / quarter)

    def rot(a, pos):
        ang = pos.astype(jnp.float32)[:, None] * freqs[None, :]
        cos = jnp.cos(ang)[None, :, None, :]
        sin = jnp.sin(ang)[None, :, None, :]
        a1, a2 = a[..., :quarter], a[..., quarter:]
        return jnp.concatenate([a1 * cos - a2 * sin, a2 * cos + a1 * sin], axis=-1)

    out = jnp.concatenate([rot(x[..., :hd // 2], rows), rot(x[..., hd // 2:], cols)], axis=-1)
    return out.astype(x.dtype)


def dense_gqa_blocks(q, k, v):
    B, S, H, hd = q.shape
    KV = k.shape[2]
    G = H // KV
    nb = S // Q_BLOCK
    scale = 1.0 / math.sqrt(hd)
    qb = q.reshape(B, nb, Q_BLOCK, KV, G, hd).transpose(1, 0, 2, 3, 4, 5)

    def one(qblk):
        s = jnp.einsum('bqkgd,btkd->bkgqt', qblk, k, preferred_element_type=jnp.float32) * scale
        p = jax.nn.softmax(s, axis=-1).astype(v.dtype)
        return jnp.einsum('bkgqt,btkd->bqkgd', p, v)

    o = lax.map(one, qb)
    return o.transpose(1, 0, 2, 3, 4, 5).reshape(B, S, H * hd)


def ctx_attention(q, k, v, sink=None):
    B, T, H, hd = q.shape
    KV = k.shape[2]
    G = H // KV
    scale = 1.0 / math.sqrt(hd)
    qg = q.reshape(B, T, KV, G, hd)
    s = jnp.einsum('bqkgd,btkd->bkgqt', qg, k, preferred_element_type=jnp.float32) * scale
    if sink is not None:
        s_sink = jnp.broadcast_to(sink.reshape(KV, G)[None, :, :, None, None].astype(jnp.float32),
                                  (B, KV, G, T, 1))
        s = jnp.concatenate([s, s_sink], axis=-1)
    p = jax.nn.softmax(s, axis=-1)
    if sink is not None:
        p = p[..., :-1]
    o = jnp.einsum('bkgqt,btkd->bqkgd', p.astype(v.dtype), v)
    return o.reshape(B, T, H * hd)


def window_gqa_blocks(q, k, v, k_ctx, v_ctx, sink):
    B, S, H, hd = q.shape
    KV = k.shape[2]
    G = H // KV
    C = k_ctx.shape[1]
    nb = S // Q_BLOCK
    scale = 1.0 / math.sqrt(hd)
    qb = q.reshape(B, nb, Q_BLOCK, KV, G, hd).transpose(1, 0, 2, 3, 4, 5)

    def band(t):
        tp = jnp.pad(t, ((0, 0), (WINDOW, WINDOW), (0, 0), (0, 0)))
        tp = tp.reshape(B, nb + 2 * N_SIDE, Q_BLOCK, KV, hd)
        tb = jnp.concatenate([tp[:, j:j + nb] for j in range(BAND_BLOCKS)], axis=2)
        return tb.transpose(1, 0, 2, 3, 4)

    kb, vb = band(k), band(v)
    n_i = jnp.arange(nb)[:, None, None]
    q_i = jnp.arange(Q_BLOCK)[None, :, None]
    k_j = jnp.arange(BAND_LEN)[None, None, :]
    qpos = n_i * Q_BLOCK + q_i
    kpos = n_i * Q_BLOCK - WINDOW + k_j
    mask = (kpos >= 0) & (kpos < S) & (jnp.abs(qpos - kpos) <= WINDOW)
    sink_l = sink.reshape(KV, G)[None, :, :, None, None].astype(jnp.float32)

    def one(args):
        qblk, kblk, vblk, m = args
        s_w = jnp.einsum('bqkgd,btkd->bkgqt', qblk, kblk, preferred_element_type=jnp.float32) * scale
        s_w = jnp.where(m[None, None, None], s_w, -jnp.inf)
        s_c = jnp.einsum('bqkgd,btkd->bkgqt', qblk, k_ctx, preferred_element_type=jnp.float32) * scale
        s_s = jnp.broadcast_to(sink_l, (B, KV, G, Q_BLOCK, 1))
        p = jax.nn.softmax(jnp.concatenate([s_w, s_c, s_s], axis=-1), axis=-1).astype(v.dtype)
        o = jnp.einsum('bkgqt,btkd->bqkgd', p[..., :BAND_LEN], vblk)
        o = o + jnp.einsum('bkgqt,btkd->bqkgd', p[..., BAND_LEN:BAND_LEN + C], v_ctx)
        return o

    o = lax.map(one, (qb, kb, vb, mask))
    return o.transpose(1, 0, 2, 3, 4, 5).reshape(B, S, H * hd)


def centred_mean(u, w):
    S = u.shape[1]
    cs = jnp.pad(jnp.cumsum(u.astype(jnp.float32), axis=1), ((0, 0), (1, 0), (0, 0)))
    t = jnp.arange(S)
    lo = jnp.clip(t - w // 2, 0, S)
    hi = jnp.clip(t + w - w // 2, 0, S)
    cnt = (hi - lo).astype(jnp.float32)
    return ((cs[:, hi] - cs[:, lo]) / cnt[None, :, None]).astype(u.dtype)


def pool_mixer(u, w_pool, pool_scale):
    B, S, _ = u.shape
    diffs = []
    for g, w in enumerate(B_WINDOWS):
        ug = u[..., g * B_GROUP_W:(g + 1) * B_GROUP_W]
        diffs.append(centred_mean(ug, w) - ug)
    d = jnp.stack(diffs, axis=2)
    y = jnp.einsum('bsgc,gcd->bsgd', d, w_pool).reshape(B, S, B_WIDTH)
    return y * pool_scale


def even_mixer(h, hc, w_in, w_out, q_gain, k_gain, w_pool, pool_scale, rows, cols, with_ctx):
    def project(t):
        p = t @ w_in
        q, k, v, u = jnp.split(p, [A_WIDTH, A_WIDTH + A_KV_WIDTH, A_WIDTH + 2 * A_KV_WIDTH], axis=-1)
        q = rmsnorm(split_heads(q, A_Q_HEADS), q_gain)
        k = rmsnorm(split_heads(k, A_KV_HEADS), k_gain)
        return q, k, split_heads(v, A_KV_HEADS), u

    q, k, v, u = project(h)
    qc, kc, vc, uc = project(hc)
    q = axial_rope(q, rows, cols)
    k = axial_rope(k, rows, cols)
    a = dense_gqa_blocks(q, jnp.concatenate([k, kc], axis=1), jnp.concatenate([v, vc], axis=1))
    b = pool_mixer(u, w_pool, pool_scale)
    y = jnp.concatenate([a, b], axis=-1) @ w_out
    yc = None
    if with_ctx:
        ac = ctx_attention(qc, kc, vc)
        bc = pool_mixer(uc, w_pool, pool_scale)
        yc = jnp.concatenate([ac, bc], axis=-1) @ w_out
    return y, yc


def odd_mixer(h, hc, w_in, w_out, sink, rows, cols, with_ctx):
    def project(t):
        p = t @ w_in
        q, k, v = jnp.split(p, [C_WIDTH, C_WIDTH + C_KV_WIDTH], axis=-1)
        return split_heads(q, C_Q_HEADS), split_heads(k, C_KV_HEADS), split_heads(v, C_KV_HEADS)

    q, k, v = project(h)
    qc, kc, vc = project(hc)
    q = axial_rope(q, rows, cols)
    k = axial_rope(k, rows, cols)
    y = window_gqa_blocks(q, k, v, kc, vc, sink) @ w_out
    yc = None
    if with_ctx:
        yc = ctx_attention(qc, kc, vc, sink) @ w_out
    return y, yc


def swiglu(h, w_in, w_out):
    g, u = jnp.split(h @ w_in, 2, axis=-1)
    return (jax.nn.silu(g) * u) @ w_out


def setup_inputs(seed: int = 0) -> dict:
    key = jax.random.key(seed)
    ks = jax.random.split(key, 24)
    D = D_MODEL

    def nrm(k, shape, std):
        return jax.random.normal(k, shape, dtype=jnp.float32) * std

    return {
        "x": nrm(ks[0], (BATCH, SEQ, D), 1.0),
        "c": nrm(ks[1], (BATCH, D), 1.0),
        "ctx": nrm(ks[2], (BATCH, CTX_LEN, D), 1.0),
        "c_ctx": nrm(ks[3], (D,), 1.0),
        "w_mod": nrm(ks[4], (DEPTH, D, 6 * D), 0.5 * D ** -0.5),
        "b_mod": nrm(ks[5], (DEPTH, 6 * D), 0.02),
        "g_pre_mix": 1.0 + nrm(ks[6], (DEPTH, D), 0.05),
        "g_post_mix": 1.0 + nrm(ks[7], (DEPTH, D), 0.05),
        "g_pre_ffn": 1.0 + nrm(ks[8], (DEPTH, D), 0.05),
        "g_post_ffn": 1.0 + nrm(ks[9], (DEPTH, D), 0.05),
        "we_in": nrm(ks[10], (N_EVEN, D, EVEN_IN), D ** -0.5),
        "we_out": nrm(ks[11], (N_EVEN, EVEN_MIX, D), EVEN_MIX ** -0.5),
        "we_q_gain": 1.0 + nrm(ks[12], (N_EVEN, HEAD_DIM), 0.05),
        "we_k_gain": 1.0 + nrm(ks[13], (N_EVEN, HEAD_DIM), 0.05),
        "we_pool": nrm(ks[14], (N_EVEN, B_GROUPS, B_GROUP_W, B_GROUP_W), B_GROUP_W ** -0.5),
        "we_pool_scale": 1.0 + nrm(ks[15], (N_EVEN, B_WIDTH), 0.1),
        "wo_in": nrm(ks[16], (N_ODD, D, ODD_IN), D ** -0.5),
        "wo_out": nrm(ks[17], (N_ODD, C_WIDTH, D), C_WIDTH ** -0.5),
        "wo_sink": nrm(ks[18], (N_ODD, C_Q_HEADS), 0.5),
        "w_ffn_in": nrm(ks[19], (DEPTH, D, 2 * FFN_HIDDEN), D ** -0.5),
        "w_ffn_out": nrm(ks[20], (DEPTH, FFN_HIDDEN, D), FFN_HIDDEN ** -0.5),
    }


def reference(x, c, ctx, c_ctx, w_mod, b_mod, g_pre_mix, g_post_mix, g_pre_ffn, g_post_ffn,
              we_in, we_out, we_q_gain, we_k_gain, we_pool, we_pool_scale,
              wo_in, wo_out, wo_sink, w_ffn_in, w_ffn_out):
    S = x.shape[1]
    ROWS = S // GRID_W
    rows = jnp.repeat(jnp.arange(ROWS, dtype=jnp.int32), GRID_W)
    cols = jnp.tile(jnp.arange(GRID_W, dtype=jnp.int32), ROWS)
    silu_c = jax.nn.silu(c)
    silu_cc = jax.nn.silu(c_ctx)

    for l in range(DEPTH):
        with_ctx = l < DEPTH - 1
        mod = (silu_c @ w_mod[l] + b_mod[l])[:, None, :]
        mod_c = (silu_cc @ w_mod[l] + b_mod[l])[None, None, :]
        sh_m, sc_m, gt_m, sh_f, sc_f, gt_f = jnp.split(mod, 6, axis=-1)
        csh_m, csc_m, cgt_m, csh_f, csc_f, cgt_f = jnp.split(mod_c, 6, axis=-1)

        h = modulate(x, g_pre_mix[l], sh_m, sc_m)
        hc = modulate(ctx, g_pre_mix[l], csh_m, csc_m)
        i = l // 2
        if l % 2 == 0:
            y, yc = even_mixer(h, hc, we_in[i], we_out[i], we_q_gain[i], we_k_gain[i],
                               we_pool[i], we_pool_scale[i], rows, cols, with_ctx)
        else:
            y, yc = odd_mixer(h, hc, wo_in[i], wo_out[i], wo_sink[i], rows, cols, with_ctx)

        x = x + gt_m * rmsnorm(y, g_post_mix[l])
        h = modulate(x, g_pre_ffn[l], sh_f, sc_f)
        x = x + gt_f * rmsnorm(swiglu(h, w_ffn_in[l], w_ffn_out[l]), g_post_ffn[l])

        if with_ctx:
            ctx = ctx + cgt_m * rmsnorm(yc, g_post_mix[l])
            hc = modulate(ctx, g_pre_ffn[l], csh_f, csc_f)
            ctx = ctx + cgt_f * rmsnorm(swiglu(hc, w_ffn_in[l], w_ffn_out[l]), g_post_ffn[l])
    return x
```

```python
import math
import os
DBG = set(os.environ.get('KDBG', '').split(','))
from contextlib import ExitStack
from collections import deque

import numpy as np
import concourse.bass as bass
import concourse.mybir as mybir
from concourse.bass_utils import run_bass_kernel_spmd

F32 = mybir.dt.float32
BF16 = mybir.dt.bfloat16
AF = mybir.ActivationFunctionType
ALU = mybir.AluOpType

D = 1024
SEQ = 2048
CTX = 256
DEPTH = 4
NT = 18
HID = 2816
NJ = 22
EPS = 1e-6
N_CORES = 8
SAME_SYNC = True
WINS = (2, 4, 8, 16)


class SemK:
    __slots__ = ("h", "total")

    def __init__(self, h):
        self.h = h
        self.total = 0


class Tr:
    __slots__ = ("w", "r")

    def __init__(self, base=None):
        self.w = dict(base) if base else {}
        self.r = {}


class Prog:
    ENG = ("pe", "act", "dve", "pool", "sp")

    def __init__(self, nc, es):
        self.nc = nc
        self.es = es
        self.streams = {e: [] for e in self.ENG}
        self.cnt = {e: 0 for e in self.ENG}
        self.waited = {e: {} for e in self.ENG}
        self.esem = {}
        for e in ("pe", "act", "dve", "pool"):
            self.esem[e] = self.new_sem("e_" + e)
        self.trs = {}
        self.ubase = {}
        self.ukeys = set()
        self.n_inst = 0

    def new_sem(self, name):
        return SemK(self.es.enter_context(self.nc.semaphore(name)))

    def T(self, *key):
        t = self.trs.get(key)
        if t is None:
            t = Tr(self.ubase if key[0] in self.ukeys else None)
            self.trs[key] = t
        return t

    def retire(self, names):
        for key in [k for k in self.trs if k[0] in names]:
            t = self.trs.pop(key)
            for d in (t.w, t.r):
                for s, v in d.items():
                    if self.ubase.get(s, 0) < v:
                        self.ubase[s] = v

    def emit(self, eng, fn, reads=(), writes=(), signal=True, dsem=None):
        deps = {}
        for t in reads:
            for s, v in t.w.items():
                if deps.get(s, 0) < v:
                    deps[s] = v
        for t in writes:
            for d in (t.w, t.r):
                for s, v in d.items():
                    if deps.get(s, 0) < v:
                        deps[s] = v
        own = self.esem.get(eng)
        wd = self.waited[eng]
        waits = []
        for s, v in deps.items():
            if s is own and (eng == "pe" or not SAME_SYNC):
                continue
            if wd.get(s, 0) >= v:
                continue
            wd[s] = v
            waits.append((s.h, v))
        if dsem is not None:
            dsem.total += 16
            tok_s, tok_v = dsem, dsem.total
            sig = False
        else:
            tok_s, tok_v = own, self.cnt[eng] + 1
            sig = signal
            if signal:
                self.cnt[eng] += 1
        for t in reads:
            if t.r.get(tok_s, 0) < tok_v:
                t.r[tok_s] = tok_v
        for t in writes:
            t.w = {tok_s: tok_v}
            t.r = {}
        self.streams[eng].append((waits, fn, sig))
        self.n_inst += 1 + len(waits)

    def dma(self, eng, out, in_, dsem, reads=(), writes=()):
        h = dsem.h
        self.emit(eng, lambda e: e.dma_start(out=out, in_=in_).then_inc(h, 16),
                  reads=reads, writes=writes, dsem=dsem)


    def dma_group(self, items, dsem, reads=(), writes=()):
        deps = {}
        for t in reads:
            for s, v in t.w.items():
                if deps.get(s, 0) < v:
                    deps[s] = v
        for t in writes:
            for d in (t.w, t.r):
                for s, v in d.items():
                    if deps.get(s, 0) < v:
                        deps[s] = v
        h = dsem.h
        for (eng, out, in_) in items:
            wd = self.waited[eng]
            waits = []
            for s, v in deps.items():
                if wd.get(s, 0) >= v:
                    continue
                wd[s] = v
                waits.append((s.h, v))
            dsem.total += 16
            self.streams[eng].append((waits, (lambda out=out, in_=in_: lambda e: e.dma_start(out=out, in_=in_).then_inc(h, 16))(), False))
            self.n_inst += 1 + len(waits)
        for t in reads:
            if t.r.get(dsem, 0) < dsem.total:
                t.r[dsem] = dsem.total
        for t in writes:
            t.w = {dsem: dsem.total}
            t.r = {}

    def play(self, final_waits):
        nc = self.nc
        with nc.Block() as block:
            def mk(ename):
                stream = self.streams[ename]
                own = self.esem.get(ename)

                def body(e):
                    for waits, fn, sig in stream:
                        for h, v in waits:
                            e.wait_ge(h, v)
                        ins = fn(e)
                        if sig:
                            ins.then_inc(own.h, 1)
                    if ename == "sp":
                        for s in final_waits:
                            if s.total:
                                e.wait_ge(s.h, s.total)
                return body
            block.tensor(mk("pe"))
            block.scalar(mk("act"))
            block.vector(mk("dve"))
            block.gpsimd(mk("pool"))
            block.sync(mk("sp"))


def MM(out, lhsT, rhs, st=True, sp=True):
    return lambda e: e.matmul(out, lhsT, rhs, start=st, stop=sp)


def TP(out, in_, ident):
    return lambda e: e.transpose(out, in_, ident)


def ACT(out, in_, func, **kw):
    return lambda e: e.activation(out=out, in_=in_, func=func, **kw)


def ACP(out, in_):
    return lambda e: e.copy(out=out, in_=in_)


def TT(out, a, b, op):
    return lambda e: e.tensor_tensor(out=out, in0=a, in1=b, op=op)


def TS(out, in0, s1, s2=None, op0=ALU.mult, op1=None):
    if op1 is None:
        return lambda e: e.tensor_scalar(out=out, in0=in0, scalar1=s1, scalar2=None, op0=op0)
    return lambda e: e.tensor_scalar(out=out, in0=in0, scalar1=s1, scalar2=s2, op0=op0, op1=op1)


def STT(out, in0, scalar, in1, op0, op1):
    return lambda e: e.scalar_tensor_tensor(out=out, in0=in0, scalar=scalar, in1=in1, op0=op0, op1=op1)


def RCP(out, in_):
    return lambda e: e.reciprocal(out=out, in_=in_)


def VCP(out, in_):
    return lambda e: e.tensor_copy(out=out, in_=in_)


def MSET(ap, v):
    return lambda e: e.memset(ap, v)


def layer_info(l):
    even = (l % 2 == 0)
    return dict(even=even, i=l // 2, nq=4 if even else 8, nk=1 if even else 2,
                nslot_in=5 if even else 6, with_ctx=(l < DEPTH - 1))


def weight_schedule(nb, layers):
    out = []
    for b in range(nb):
        for l in layers:
            li = layer_info(l)
            for n in range(5):
                for s in range(li["nslot_in"]):
                    want_q = (n < 4) or li["with_ctx"]
                    if not want_q and s < li["nq"] // 2:
                        continue
                    if not want_q and li["even"] and s >= 3:
                        continue
                    out.append(("in", l, n, s))
            for s in range(4):
                out.append(("out", l, 0, s))
            nblk = 5 if li["with_ctx"] else 4
            for n in range(nblk):
                for j in range(NJ):
                    out.append(("ffn", l, n, j))
    return out


def build_program(nb, layers):
    NB1 = nb + 1
    nc = bass.Bass("TRN2", target_bir_lowering=False)
    es = ExitStack()

    def din(name, shape, dt=F32):
        return nc.dram_tensor(name, list(shape), dt, kind="ExternalInput").ap()

    x_d = din("x", [nb, SEQ, D])
    ctx_d = din("ctx", [nb, CTX, D])
    scT_d = din("scT", [128, 8, NB1])
    wmod_d = din("wmod", [DEPTH, 12, 128, 8, 512])
    bmodT_d = din("bmodT", [128, DEPTH, 48])
    bgate_d = din("bgate", [NB1, DEPTH, 2, D])
    gpost_d = din("gpost", [NB1, DEPTH, 2, D])
    gpreT_d = din("gpreT", [128, DEPTH, 2, 8])
    wein_d = din("wein", [2, 5, 128, 8, 256])
    weout_d = din("weout", [2, 4, 128, 8, 256])
    wepool_d = din("wepool", [128, 2, 4, 128])
    wegain_d = din("wegain", [128, 2, 2])
    wepsc_d = din("wepsc", [128, 2, 4])
    woin_d = din("woin", [2, 6, 128, 8, 256])
    woout_d = din("woout", [2, 4, 128, 8, 256])
    wosink_d = din("wosink", [128, 2, 16])
    wfin_d = din("wfin", [DEPTH, NJ, 128, 8, 256])
    wfout_d = din("wfout", [DEPTH, 128, NJ, D])
    cos_d = din("cosT", [128, SEQ])
    sin_d = din("sinT", [128, SEQ])
    cmat_d = din("cmat", [128, 5, 128])
    mband_d = din("mband", [128, 4, 5, 128])
    out_d = nc.dram_tensor("out", [nb, SEQ, D], F32, kind="ExternalOutput").ap()
    gsc_d = nc.dram_tensor("gsc", [DEPTH, 2, NB1, D], F32, kind="Internal").ap()

    P = Prog(nc, es)
    T = P.T

    def sb(name, shape, dt):
        return es.enter_context(nc.sbuf_tensor(name, list(shape), dt))

    def ps(name, shape, dt):
        return es.enter_context(nc.psum_tensor(name, list(shape), dt))

    xs = sb("xs", [128, NT, D], F32)
    U = sb("U", [128, 16896], F32)
    hT = sb("hT", [128, 8, 512], BF16)
    NSLOT = 4
    wsl = sb("wsl", [128, NSLOT, 8, 256], BF16)
    cos_sb = sb("cos_sb", [128, SEQ], BF16)
    sin_sb = sb("sin_sb", [128, SEQ], BF16)
    GG = sb("GG", [128, 2, D], F32)
    xn2 = sb("xn2", [128, 2, D], BF16)
    tmp = sb("tmp", [128, D], F32)
    tmpb = tmp[:, :].bitcast(BF16)
    PT = sb("PT", [128, 3, 512], BF16)
    r_sd = sb("r_sd", [128, 512], F32)
    r_qn = sb("r_qn", [128, 512], BF16)
    r_t1 = sb("r_t1", [128, 512], F32)
    r_t2 = sb("r_t2", [128, 512], F32)
    r_sq = r_qn
    r_t2b = r_t2[:, :].bitcast(BF16)
    osb = r_t1
    rcp = r_sd
    cmat = sb("cmat_sb", [128, 5, 128], BF16)
    mband = sb("mband_sb", [128, 4, 5, 128], BF16)
    wpool = sb("wpool_sb", [128, 4, 128], BF16)
    onesel = sb("onesel", [128, 2, 128], BF16)
    stat = sb("stat", [128, 3, 4], F32)
    scT = sb("scT_sb", [128, 8, NB1], F32)
    AT = sb("AT", [128, DEPTH, 2, 8, NB1], F32)
    BT = sb("BT", [128, DEPTH, 2, 8, NB1], F32)
    wegain = sb("wegain_sb", [128, 2, 2], F32)
    wepsc = sb("wepsc_sb", [128, 2, 4], F32)
    esink = sb("esink", [128, 2, 16], F32)

    Ub = U[:, :].bitcast(BF16)
    QA = Ub[:, 0:18432].rearrange("p (c t) -> p c t", c=8)
    KT = Ub[:, 18432:23040].rearrange("p (c t) -> p c t", c=2)
    VA = Ub[:, 23040:29952].rearrange("p (t v) -> p t v", t=NT)
    KTe = Ub[:, 18432:20736].rearrange("p (c t) -> p c t", c=1)
    VAe = Ub[:, 20736:24192].rearrange("p (t v) -> p t v", t=NT)
    UT = Ub[:, 24192:33408].rearrange("p (t v) -> p t v", t=NT)
    w2 = Ub[:, 0:22528].rearrange("p (j n) -> p j n", j=NJ)
    mT = Ub[:, 22528:33792].rearrange("p (j n) -> p j n", j=NJ)
    wmst = U[:, 0:8192].rearrange("p (s k n) -> p s k n", s=2, k=8)
    grow = U[0:NB1, 8192:9728].rearrange("p (a n) -> p a n", a=3)
    modT = U[:, 9728:9728 + DEPTH * 48 * NB1].rearrange("p (l c b) -> p l c b", l=DEPTH, c=48)
    o_ = 9728 + DEPTH * 48 * NB1
    bmodT = U[:, o_:o_ + DEPTH * 48].rearrange("p (l c) -> p l c", l=DEPTH)
    gpreT = U[:, o_ + DEPTH * 48:o_ + DEPTH * 48 + DEPTH * 16].rearrange("p (l w c) -> p l w c", l=DEPTH, w=2)
    P.ukeys = {"QA", "KT", "VA", "UT", "w2", "mT", "wmst", "grow", "modT", "stc"}
    MIX_KEYS = {"QA", "KT", "VA", "UT"}
    FFN_KEYS = {"w2", "mT"}

    psT = ps("psT", [128, 8, 128], BF16)
    psA = ps("psA", [128, 512], F32)
    psB = ps("psB", [128, 1024], F32)
    psC = ps("psC", [128, 1024], F32)
    psD = ps("psD", [128, 1024], F32)
    bankB = [(psB[:, 0:512], ("psB", 0)), (psB[:, 512:1024], ("psB", 1))]
    bankC = [(psC[:, 0:512], ("psC", 0)), (psC[:, 512:1024], ("psC", 1))]
    bankS = bankB + [(psD[:, 0:512], ("psD", 0)), (psD[:, 512:1024], ("psD", 1))]

    s_const = P.new_sem("s_const")
    s_constb = P.new_sem("s_constb")
    s_x = [P.new_sem("s_x%d" % t) for t in range(NT)]
    s_out = [P.new_sem("s_o%d" % t) for t in range(16)]
    s_slot = [P.new_sem("s_ws%d" % s) for s in range(NSLOT)]
    s_w2 = P.new_sem("s_w2")
    s_wm = [P.new_sem("s_wm0"), P.new_sem("s_wm1")]
    s_gg = [P.new_sem("s_gg0"), P.new_sem("s_gg1")]
    s_gs = P.new_sem("s_gs")
    s_wp = P.new_sem("s_wp")
    s_stc = P.new_sem("s_stc")
    s_gr = [P.new_sem("s_gr1"), P.new_sem("s_gr2")]

    ident = cmat[:, 0, :]
    perm = cmat[:, 1, :]
    blkm = cmat[:, 2, :]
    maskL = cmat[:, 3, :]
    maskR = cmat[:, 4, :]

    cT = T("const")
    P.dma_group([
        ("pool", cmat[:], cmat_d[:, :, :]),
        ("pool", mband[:], mband_d[:, :, :, :]),
        ("pool", cos_sb[:], cos_d[:, :]),
        ("pool", sin_sb[:], sin_d[:, :]),
    ], s_const, writes=[cT])
    cTb = T("constb")
    P.dma_group([
        ("sp", scT[:], scT_d[:, :, :]),
        ("sp", wegain[:], wegain_d[:, :, :]),
        ("sp", wepsc[:], wepsc_d[:, :, :]),
        ("sp", esink[:], wosink_d[:, :, :]),
    ], s_constb, writes=[cTb])
    P.emit("act", ACT(scT[:], scT[:], AF.Silu), reads=[cTb], writes=[cTb])
    P.emit("act", ACT(esink[:], esink[:], AF.Exp), reads=[cTb], writes=[cTb])
    P.emit("act", ACP(stat[:, 0, 0:1], esink[:, 0, 0:1]), reads=[cT, cTb], writes=[cT])
    P.dma_group([("sp", bmodT, bmodT_d[:, :, :]), ("sp", gpreT, gpreT_d[:, :, :, :])], s_stc, writes=[T("stc")])
    c2 = T("const2")
    P.emit("dve", MSET(onesel[:], 0.0), writes=[c2])
    P.emit("dve", MSET(onesel[:, 0, 0:64], 1.0), writes=[c2])
    P.emit("dve", MSET(onesel[:, 1, 64:128], 1.0), writes=[c2])

    GATE_CB = {4: (0, 0), 5: (0, 1), 10: (1, 0), 11: (1, 1)}
    cnt_wm = 0
    for l in layers:
        for cb in range(12):
            sl = cnt_wm % 2
            cnt_wm += 1
            q = "sp" if cnt_wm % 2 else "act"
            P.dma(q, wmst[:, sl], wmod_d[l, cb], s_wm[sl], writes=[T("wmst", sl)])
            if cb in GATE_CB:
                which, half = GATE_CB[cb]
                hs = slice(half * 512, (half + 1) * 512)
                for kc in range(8):
                    P.emit("pe", MM(psB[0:NB1, 0:512], scT[:, kc, :], wmst[:, sl, kc, :], kc == 0, kc == 7),
                           reads=[cT, T("wmst", sl)], writes=[T("psB", 0)], signal=(kc == 7))
                P.dma("sp", grow[:, 1, :], bgate_d[:, l, which, hs], s_gr[0], writes=[T("grow", 1)])
                P.dma("sp", grow[:, 2, :], gpost_d[:, l, which, hs], s_gr[1], writes=[T("grow", 2)])
                P.emit("dve", TT(grow[:, 0, :], psB[0:NB1, 0:512], grow[:, 1, :], ALU.add),
                       reads=[T("psB", 0), T("grow", 1)], writes=[T("grow", 0)])
                P.emit("dve", TT(grow[:, 0, :], grow[:, 0, :], grow[:, 2, :], ALU.mult),
                       reads=[T("grow", 2)], writes=[T("grow", 0)])
                P.dma("sp", gsc_d[l, which, :, hs], grow[:, 0, :], s_gs, reads=[T("grow", 0)], writes=[T("gsc")])
            else:
                for fc in range(4):
                    ch = cb * 4 + fc
                    pa = psA[:, fc * NB1:(fc + 1) * NB1]
                    for kc in range(8):
                        P.emit("pe", MM(pa, wmst[:, sl, kc, fc * 128:(fc + 1) * 128], scT[:, kc, :], kc == 0, kc == 7),
                               reads=[cT, T("wmst", sl)], writes=[T("psA")], signal=(kc == 7))
                    P.emit("dve", TS(modT[:, l, ch, :], pa, bmodT[:, l, ch:ch + 1], op0=ALU.add),
                           reads=[T("psA"), T("stc")], writes=[T("modT")])
        for which, sc0 in ((0, 8), (1, 32)):
            for c in range(8):
                P.emit("dve", TS(AT[:, l, which, c, :], modT[:, l, sc0 + c, :], 1.0, gpreT[:, l, which, c:c + 1],
                                 op0=ALU.add, op1=ALU.mult),
                       reads=[T("modT"), T("stc")], writes=[T("AT")])
            sh0_ = 0 if which == 0 else 24
            P.emit("dve", VCP(BT[:, l, which, :, :], modT[:, l, sh0_:sh0_ + 8, :]), reads=[T("modT")], writes=[T("AT")])
    P.retire({"wmst", "grow", "modT", "stc"})

    sched = deque(weight_schedule(nb, layers))
    inflight = deque()
    ring = {"issued": 0, "released": 0, "popped": 0}

    def w_src(key):
        kind, l, n, s = key
        li = layer_info(l)
        if kind == "in":
            return (wein_d if li["even"] else woin_d)[li["i"], s]
        if kind == "out":
            return (weout_d if li["even"] else woout_d)[li["i"], s]
        return wfin_d[l, s]

    def w_prefetch():
        while sched and ring["issued"] - ring["released"] < NSLOT:
            key = sched.popleft()
            s = ring["issued"] % NSLOT
            ring["issued"] += 1
            P.dma("pool", wsl[:, s], w_src(key), s_slot[s], writes=[T("wsl", s)])
            inflight.append((key, s))

    def w_pop(kind, l, n, s):
        w_prefetch()
        key, slot = inflight.popleft()
        assert key == (kind, l, n, s), (key, (kind, l, n, s))
        ring["popped"] += 1
        return slot

    def w_release(k=1):
        ring["released"] += k
        assert ring["released"] <= ring["popped"]
        w_prefetch()

    ev = {"i": 0}

    def blocks_of():
        bl = [(n, n * 512, 512, list(range(4 * n, 4 * n + 4))) for n in range(4)]
        bl.append((4, SEQ, 256, [16, 17]))
        return bl

    xn_state = {"i": 0}

    def norm_pre(b, l, which, t):
        col = ev["i"] % 4
        ev["i"] += 1
        xi = xn_state["i"] % 2
        xn_state["i"] += 1
        xb = xn2[:, xi, :]
        xt = T("x", t)
        sT_ = T("stat", col)
        P.emit("act", ACT(xb, xs[:, t, :], AF.Square, accum_out=stat[:, 0, col:col + 1]),
               reads=[xt], writes=[sT_, T("xn", xi)])
        P.emit("act", ACT(stat[:, 1, col:col + 1], stat[:, 0, col:col + 1], AF.Sqrt, scale=1.0 / D, bias=eps_ap),
               reads=[c2], writes=[sT_])
        P.emit("dve", RCP(stat[:, 2, col:col + 1], stat[:, 1, col:col + 1]), writes=[sT_])
        P.emit("dve", TS(xb, xs[:, t, :], stat[:, 2, col:col + 1]), reads=[xt, sT_], writes=[T("xn", xi)])
        return xi

    def norm_post(b, l, which, t, k, xi):
        bsel = nb if t >= 16 else b
        for c in range(8):
            P.emit("pe", TP(psT[:, c, :], xn2[:, xi, c * 128:(c + 1) * 128], ident),
                   reads=[T("xn", xi), cT], writes=[T("psT")], signal=(c == 7))
        for c in range(8):
            a_ap = AT[:, l, which, c, bsel:bsel + 1]
            b_ap = BT[:, l, which, c, bsel:bsel + 1]
            dst = hT[:, c, k * 128:(k + 1) * 128]
            if c % 2 == 0:
                P.emit("act", ACT(dst, psT[:, c, :], AF.Identity, scale=a_ap, bias=b_ap),
                       reads=[T("psT"), T("AT")], writes=[T("hT", k)])
            else:
                P.emit("dve", TS(dst, psT[:, c, :], a_ap, b_ap, op0=ALU.mult, op1=ALU.add),
                       reads=[T("psT"), T("AT")], writes=[T("hT", k)])

    def norm_block(b, l, which, tiles):
        xi = norm_pre(b, l, which, tiles[0])
        for k, t in enumerate(tiles):
            nxt = norm_pre(b, l, which, tiles[k + 1]) if k + 1 < len(tiles) else None
            norm_post(b, l, which, t, k, xi)
            xi = nxt

    pp_state = {"i": 0}

    def proj_fm(slot, off, N, ntl):
        ap_, key = bankB[pp_state["i"] % 2]
        pp_state["i"] += 1
        for kc in range(8):
            P.emit("pe", MM(ap_[:, 0:N], wsl[:, slot, kc, off:off + 128], hT[:, kc, 0:N], kc == 0, kc == 7),
                   reads=[T("wsl", slot)] + [T("hT", k) for k in range(ntl)], writes=[T(*key)], signal=(kc == 7))
        return ap_, key

    def qk_post(li, pp, pkey, N, s0, is_ctx, gain_ap, dst, dst_tr):
        ppt = T(*pkey)
        if 'noqk' in DBG:
            return
        cs = cos_sb[:, s0:s0 + N] if not is_ctx else None
        sn = sin_sb[:, s0:s0 + N] if not is_ctx else None
        if li["even"]:
            sqb = PT[:, 0, :]
            P.emit("act", ACT(sqb[:, 0:N], pp[:, 0:N], AF.Square), reads=[ppt], writes=[T("PT", 0)])
            P.emit("dve", TS(r_qn[:, 0:N], pp[:, 0:N], gain_ap), reads=[ppt, cT, T("PT", 0)], writes=[T("r_qn")])
            P.emit("pe", MM(psC[:, 0:N], blkm, sqb[:, 0:N]), reads=[T("PT", 0), cT], writes=[T("psC", 0)])
            if not is_ctx:
                P.emit("pe", MM(psC[:, 512:512 + N], perm, r_qn[:, 0:N]), reads=[T("r_qn"), cT], writes=[T("psC", 1)])
            P.emit("act", ACT(r_sd[:, 0:N], psC[:, 0:N], AF.Sqrt, bias=eps_ap, scale=1.0),
                   reads=[T("psC", 0), c2], writes=[T("r_sd")])
            P.emit("dve", RCP(r_sd[:, 0:N], r_sd[:, 0:N]), writes=[T("r_sd")])
            if is_ctx:
                P.emit("dve", TT(dst, r_qn[:, 0:N], r_sd[:, 0:N], ALU.mult), reads=[T("r_qn"), T("r_sd")], writes=[dst_tr])
                return
            P.emit("dve", TT(r_t1[:, 0:N], r_qn[:, 0:N], cs, ALU.mult), reads=[T("r_qn"), cT], writes=[T("r_t1")])
            P.emit("dve", TT(r_t2[:, 0:N], psC[:, 512:512 + N], sn, ALU.mult), reads=[T("psC", 1), cT], writes=[T("r_t2")])
            P.emit("dve", TT(r_t1[:, 0:N], r_t1[:, 0:N], r_t2[:, 0:N], ALU.add), reads=[T("r_t2")], writes=[T("r_t1")])
            P.emit("dve", TT(dst, r_t1[:, 0:N], r_sd[:, 0:N], ALU.mult), reads=[T("r_t1"), T("r_sd")], writes=[dst_tr])
            return
        else:
            if is_ctx:
                P.emit("act", ACP(dst, pp[:, 0:N]), reads=[ppt], writes=[dst_tr])
                return
            P.emit("act", ACP(r_qn[:, 0:N], pp[:, 0:N]), reads=[ppt], writes=[T("r_qn")])
            P.emit("pe", MM(psC[:, 512:512 + N], perm, r_qn[:, 0:N]), reads=[T("r_qn"), cT], writes=[T("psC", 1)])
            P.emit("dve", TT(r_t1[:, 0:N], r_qn[:, 0:N], cs, ALU.mult), reads=[T("r_qn"), cT], writes=[T("r_t1")])
        P.emit("dve", TT(r_t2[:, 0:N], psC[:, 512:512 + N], sn, ALU.mult), reads=[T("psC", 1), cT], writes=[T("r_t2")])
        P.emit("dve", TT(dst, r_t1[:, 0:N], r_t2[:, 0:N], ALU.add), reads=[T("r_t1"), T("r_t2")], writes=[dst_tr])

    def in_proj_block(b, l, li, blk):
        n, s0, N, tiles = blk
        is_ctx = (n == 4)
        ntl = len(tiles)
        want_q = (not is_ctx) or li["with_ctx"]
        even = li["even"]
        i = li["i"]
        kt = KTe if even else KT
        va = VAe if even else VA
        nqs = li["nq"] // 2
        st = {"vslot": None, "voff": 0, "VW": 0}

        def chunk_iter():
            if want_q:
                for s in range(nqs):
                    slot = w_pop("in", l, n, s)
                    for hf in range(2):
                        c = 2 * s + hf
                        yield (slot, hf * 128, hf == 1,
                               (wegain[:, i, 0:1] if even else None, QA[:, c, s0:s0 + N], T("QA", c, n)))
            if even:
                st["vslot"] = w_pop("in", l, n, 2)
                st["voff"], st["VW"] = 128, 128
                yield (st["vslot"], 0, False, (wegain[:, i, 1:2], kt[:, 0, s0:s0 + N], T("KT", 0, n)))
            else:
                slot = w_pop("in", l, n, 4)
                for hf in range(2):
                    yield (slot, hf * 128, hf == 1, (None, kt[:, hf, s0:s0 + N], T("KT", hf, n)))

        prev = None
        for (slot, off, rel, pa) in chunk_iter():
            pp, pkey = proj_fm(slot, off, N, ntl)
            if rel:
                w_release()
            if prev is not None:
                qk_post(*prev)
            prev = (li, pp, pkey, N, s0, is_ctx) + pa
        if not even:
            st["vslot"] = w_pop("in", l, n, 5)
            st["voff"], st["VW"] = 0, 256
        vslot, voff, VW = st["vslot"], st["voff"], st["VW"]
        for k, t in enumerate(tiles):
            if k == 1 and prev is not None:
                qk_post(*prev)
                prev = None
            for kc in range(8):
                P.emit("pe", MM(psA[:, 0:VW], hT[:, kc, k * 128:(k + 1) * 128], wsl[:, vslot, kc, voff:voff + VW],
                                kc == 0, kc == 7),
                       reads=[T("wsl", vslot), T("hT", k)], writes=[T("psA")], signal=(kc == 7))
            if even:
                P.emit("act", ACP(va[:, t, 0:64], psA[:, 0:64]), reads=[T("psA")], writes=[T("VA", t)])
                P.emit("dve", VCP(va[:, t, 128:192], psA[:, 64:128]), reads=[T("psA")], writes=[T("VA", t)])
            else:
                for gi, vb_ in enumerate((0, 128, 192, 320)):
                    if gi % 2 == 0:
                        P.emit("act", ACP(va[:, t, vb_:vb_ + 64], psA[:, gi * 64:(gi + 1) * 64]), reads=[T("psA")], writes=[T("VA", t)])
                    else:
                        P.emit("dve", VCP(va[:, t, vb_:vb_ + 64], psA[:, gi * 64:(gi + 1) * 64]), reads=[T("psA")], writes=[T("VA", t)])
        if prev is not None:
            qk_post(*prev)
            prev = None
        w_release()
        if even and want_q:
            s3 = w_pop("in", l, n, 3)
            s4 = w_pop("in", l, n, 4)
            for k, t in enumerate(tiles):
                ap_, key = bankB[pp_state["i"] % 2]
                pp_state["i"] += 1
                for hs, slot in enumerate((s3, s4)):
                    for kc in range(8):
                        P.emit("pe", MM(ap_[:, hs * 256:(hs + 1) * 256], hT[:, kc, k * 128:(k + 1) * 128],
                                        wsl[:, slot, kc, :], kc == 0, kc == 7),
                               reads=[T("wsl", slot), T("hT", k)], writes=[T(*key)], signal=(kc == 7))
                if k % 2 == 0:
                    P.emit("act", ACP(UT[:, t, :], ap_[:, 0:512]), reads=[T(*key)], writes=[T("UT", t)])
                else:
                    P.emit("dve", VCP(UT[:, t, :], ap_[:, 0:512]), reads=[T(*key)], writes=[T("UT", t)])
            w_release(2)

    def pool_mixer(b, l, li):
        i = li["i"]
        P.dma("pool", wpool[:], wepool_d[:, i, :, :], s_wp, writes=[T("wpool")])
        for g in range(4):
            for blk in blocks_of():
                n, s0, N, tiles = blk
                if n == 4 and not li["with_ctx"]:
                    continue
                first, last = (16, 17) if n == 4 else (0, 15)
                ap_, key = bankB[pp_state["i"] % 2]
                pp_state["i"] += 1
                for k, t in enumerate(tiles):
                    contrib = []
                    if t > first:
                        contrib.append((t - 1, 3))
                    contrib.append((t, 0 if t == first else (2 if t == last else 1)))
                    if t < last:
                        contrib.append((t + 1, 4))
                    for ci, (s, kind) in enumerate(contrib):
                        lastmm = (ci == len(contrib) - 1)
                        P.emit("pe", MM(ap_[:, k * 128:(k + 1) * 128], UT[:, s, g * 128:(g + 1) * 128],
                                        mband[:, g, kind, :], ci == 0, lastmm),
                               reads=[T("UT", s), cT], writes=[T(*key)], signal=(lastmm and k == len(tiles) - 1))
                P.emit("act", ACP(r_qn[:, 0:N], ap_[:, 0:N]), reads=[T(*key)], writes=[T("r_qn")])
                P.emit("pe", MM(psC[:, 0:N], wpool[:, g, :], r_qn[:, 0:N]), reads=[T("r_qn"), T("wpool")], writes=[T("psC", 0)])
                P.emit("dve", TS(QA[:, 4 + g, s0:s0 + N], psC[:, 0:N], wepsc[:, i, g:g + 1]),
                       reads=[T("psC", 0), cT], writes=[T("QA", 4 + g, n)])

    at_state = {"s": 0, "o": 0, "pt": 0}

    def attention(b, l, li):
        even = li["even"]
        i = li["i"]
        kt = KTe if even else KT
        va = VAe if even else VA
        units = []
        for c in range(li["nq"]):
            for p in range(2):
                for blk in blocks_of():
                    if blk[0] == 4 and not li["with_ctx"]:
                        continue
                    units.append((c, p, blk))

        def unit_setup(c, p, blk):
            n, s0, N, tiles = blk
            u = dict(c=c, p=p, n=n, s0=s0, N=N, p0=p * 64)
            if even:
                u["kch"], g, u["head"] = 0, p, c + 4 * p
                u["vbase"] = 0 if g == 0 else 64
            else:
                kk, r = c // 4, c % 4
                g = 2 * kk + p
                u["head"] = 8 * kk + r + 4 * p
                u["kch"] = kk
                u["vbase"] = (0, 64, 192, 256)[g]
            u["M"] = 128
            u["dp"] = 64 if p == 0 else 0
            if n == 4:
                kl = [(16, 0, N, None), (17, 0, N, None)]
            elif even:
                kl = [(j, 0, N, None) for j in range(NT)]
            else:
                kl = [(16, 0, N, None), (17, 0, N, None)]
                t0 = 4 * n
                for j in range(max(t0 - 1, 0), min(t0 + 4, 15) + 1):
                    lo = max(j - 1, t0)
                    hi = min(j + 1, t0 + 3)
                    masks = []
                    if j - 1 >= t0:
                        masks.append((j - 1 - lo, maskR))
                    if j + 1 <= t0 + 3:
                        masks.append((j + 1 - lo, maskL))
                    kl.append((j, (lo - t0) * 128, (hi - t0 + 1) * 128, masks))
            u["kl"] = kl
            u["po"], u["pokey"] = bankC[at_state["o"] % 2]
            at_state["o"] += 1
            u["qtr"] = T("QA", c, n, p)
            return u

        def emit_S(u, idx):
            j, qa, qb, masks = u["kl"][idx]
            c, n, p0, s0, kch = u["c"], u["n"], u["p0"], u["s0"], u["kch"]
            pS, pSkey = bankS[at_state["s"] % 4]
            at_state["s"] += 1
            pti = at_state["pt"] % 3
            at_state["pt"] += 1
            W = qb - qa
            P.emit("pe", MM(pS[:, 0:W], kt[p0:p0 + 64, kch, j * 128:(j + 1) * 128],
                            QA[p0:p0 + 64, c, s0 + qa:s0 + qb]),
                   reads=[T("KT", kch, 4 if j >= 16 else j // 4), T("QA", c, n), u["qtr"]], writes=[T(*pSkey)])
            P.emit("act", ACT(PT[:, pti, 0:W], pS[:, 0:W], AF.Exp, scale=0.125),
                   reads=[T(*pSkey)], writes=[T("PT", pti)])
            for (mt, mk) in (masks or []):
                P.emit("dve", TT(PT[:, pti, mt * 128:(mt + 1) * 128], PT[:, pti, mt * 128:(mt + 1) * 128], mk, ALU.mult),
                       reads=[cT], writes=[T("PT", pti)])
            return pti

        def emit_PV(u, idx, pti):
            j, qa, qb, masks = u["kl"][idx]
            W = qb - qa
            lastk = (idx == len(u["kl"]) - 1)
            M, vbase = u["M"], u["vbase"]
            P.emit("pe", MM(u["po"][0:M, qa:qb], va[:, j, vbase:vbase + M], PT[:, pti, 0:W], idx == 0, lastk),
                   reads=[T("VA", j), T("PT", pti)], writes=[T(*u["pokey"])], signal=lastk)

        def emit_norm(u):
            po, pokey, N, dp, p0, p, c, s0 = u["po"], u["pokey"], u["N"], u["dp"], u["p0"], u["p"], u["c"], u["s0"]
            if even:
                P.emit("dve", RCP(rcp[dp:dp + 1, 0:N], po[dp:dp + 1, 0:N]), reads=[T(*pokey)], writes=[T("r_sd")])
            else:
                P.emit("dve", TS(rcp[dp:dp + 1, 0:N], po[dp:dp + 1, 0:N], esink[dp:dp + 1, i, u["head"]:u["head"] + 1], op0=ALU.add),
                       reads=[T(*pokey), cT], writes=[T("r_sd")])
                P.emit("dve", RCP(rcp[dp:dp + 1, 0:N], rcp[dp:dp + 1, 0:N]), writes=[T("r_sd")])
            rhi = r_qn[dp:dp + 1, 0:N]
            rlo = r_t2b[dp:dp + 1, 0:N]
            P.emit("dve", VCP(rhi, rcp[dp:dp + 1, 0:N]), reads=[T("r_sd")], writes=[T("r_qn")])
            P.emit("dve", TT(rlo, rcp[dp:dp + 1, 0:N], rhi, ALU.subtract), reads=[T("r_sd"), T("r_qn")], writes=[T("r_t2")])
            P.emit("pe", MM(psA[:, 0:N], onesel[dp:dp + 1, p, :], rhi, True, False),
                   reads=[T("r_qn"), c2], writes=[T("psA")], signal=False)
            P.emit("pe", MM(psA[:, 0:N], onesel[dp:dp + 1, p, :], rlo, False, True),
                   reads=[T("r_t2"), c2], writes=[T("psA")])
            P.emit("act", ACP(osb[p0:p0 + 64, 0:N], po[p0:p0 + 64, 0:N]), reads=[T(*pokey)], writes=[T("r_t1")])
            P.emit("dve", TT(QA[p0:p0 + 64, c, s0:s0 + N], osb[p0:p0 + 64, 0:N], psA[p0:p0 + 64, 0:N], ALU.mult),
                   reads=[T("r_t1"), T("psA")], writes=[u["qtr"]])

        prev_u = None
        for (c, p, blk) in units:
            u = unit_setup(c, p, blk)
            nk = len(u["kl"])
            pts = [emit_S(u, ii) for ii in range(min(2, nk))]
            for idx in range(nk):
                if idx + 2 < nk:
                    pts.append(emit_S(u, idx + 2))
                emit_PV(u, idx, pts[idx])
                if idx == min(1, nk - 1) and prev_u is not None:
                    emit_norm(prev_u)
            prev_u = u
        if prev_u is not None:
            emit_norm(prev_u)

    gg_state = {"i": 0}

    def load_gg(l, which, bsel):
        k = gg_state["i"] % 2
        gg_state["i"] += 1
        P.dma("sp", GG[:, k, :], gsc_d[l, which, bsel, :].partition_broadcast(128), s_gg[k],
              reads=[T("gsc")], writes=[T("GG", k)])
        return k

    y_state = {"i": 0}

    def resid(t, py, pykeys, ggk):
        col = ev["i"] % 4
        ev["i"] += 1
        sT_ = T("stat", col)
        ptr = [T(*k) for k in pykeys]
        P.emit("act", ACT(tmpb[:, 0:D], py[:, 0:D], AF.Square, accum_out=stat[:, 0, col:col + 1]), reads=ptr, writes=[sT_, T("tmp")])
        P.emit("act", ACT(stat[:, 1, col:col + 1], stat[:, 0, col:col + 1], AF.Sqrt, scale=1.0 / D, bias=eps_ap), reads=[c2], writes=[sT_])
        P.emit("dve", RCP(stat[:, 2, col:col + 1], stat[:, 1, col:col + 1]), writes=[sT_])
        P.emit("dve", STT(tmp[:], py[:, 0:D], stat[:, 2, col:col + 1], GG[:, ggk, :], ALU.mult, ALU.mult),
               reads=ptr + [sT_, T("GG", ggk)], writes=[T("tmp")])
        P.emit("dve", TT(xs[:, t, :], xs[:, t, :], tmp[:], ALU.add), reads=[T("tmp")], writes=[T("x", t)])

    def y_banks():
        k = y_state["i"] % 2
        y_state["i"] += 1
        if k == 0:
            return psD, [("psD", 0), ("psD", 1)]
        return psC, [("psC", 0), ("psC", 1)]

    def out_proj(b, l, li):
        slots = [w_pop("out", l, 0, s) for s in range(4)]
        ntile = NT if li["with_ctx"] else 16
        gk_lat = load_gg(l, 0, b)
        gk_ctx = load_gg(l, 0, nb) if li["with_ctx"] else None
        for t in range(ntile):
            py, pykeys = y_banks()
            n = 4 if t >= 16 else t // 4
            for q4 in range(4):
                for kc in range(8):
                    rd = [T("wsl", slots[q4]), T("QA", kc, n)]
                    if kc < li["nq"]:
                        rd += [T("QA", kc, n, 0), T("QA", kc, n, 1)]
                    P.emit("pe", MM(py[:, q4 * 256:(q4 + 1) * 256], QA[:, kc, t * 128:(t + 1) * 128],
                                    wsl[:, slots[q4], kc, :], kc == 0, kc == 7),
                           reads=rd, writes=[T(*pykeys[q4 // 2])], signal=(kc == 7))
            resid(t, py, pykeys, gk_ctx if t >= 16 else gk_lat)
        w_release(4)

    gu_state = {"i": 0}

    def ffn(b, l, li):
        P.retire(MIX_KEYS)
        P.dma_group([("pool", w2[:, j, :], wfout_d[l, :, j, :]) for j in range(NJ)], s_w2,
                    writes=[T("w2", j) for j in range(NJ)])
        gk_lat = load_gg(l, 1, b)
        gk_ctx = load_gg(l, 1, nb) if li["with_ctx"] else None
        fblocks = [blk for blk in blocks_of() if not (blk[0] == 4 and not li["with_ctx"])]
        norm_block(b, l, 1, fblocks[0][3])
        for bi, blk in enumerate(fblocks):
            n, s0, N, tiles = blk
            ntl = len(tiles)
            ntiles_next = fblocks[bi + 1][3] if bi + 1 < len(fblocks) else []
            for j in range(NJ):
                slot = w_pop("ffn", l, n, j)
                k = gu_state["i"] % 2
                gu_state["i"] += 1
                (pg, pgk), (pu, puk) = (bankB if k == 0 else bankC)
                for (pa, pk, off) in ((pg, pgk, 0), (pu, puk, 128)):
                    for kc in range(8):
                        P.emit("pe", MM(pa[:, 0:N], wsl[:, slot, kc, off:off + 128], hT[:, kc, 0:N], kc == 0, kc == 7),
                               reads=[T("wsl", slot)] + [T("hT", kk) for kk in range(ntl)], writes=[T(*pk)], signal=(kc == 7))
                w_release()
                sg, sgk = (r_t1, "r_t1") if k == 0 else (r_t2, "r_t2")
                P.emit("act", ACT(sg[:, 0:N], pg[:, 0:N], AF.Silu), reads=[T(*pgk)], writes=[T(sgk)])
                P.emit("dve", TT(mT[:, j, 0:N], sg[:, 0:N], pu[:, 0:N], ALU.mult), reads=[T(sgk), T(*puk)], writes=[T("mT", j)])
            for k, t in enumerate(tiles):
                xi = norm_pre(b, l, 1, ntiles_next[k]) if k < len(ntiles_next) else None
                py, pykeys = y_banks()
                for f in range(2):
                    for j in range(NJ):
                        P.emit("pe", MM(py[:, f * 512:(f + 1) * 512], mT[:, j, k * 128:(k + 1) * 128],
                                        w2[:, j, f * 512:(f + 1) * 512], j == 0, j == NJ - 1),
                               reads=[T("mT", j), T("w2", j)], writes=[T(*pykeys[f])], signal=(j == NJ - 1))
                resid(t, py, pykeys, gk_ctx if t >= 16 else gk_lat)
                if xi is not None:
                    norm_post(b, l, 1, ntiles_next[k], k, xi)
        P.retire(FFN_KEYS)

    def mixer(b, l, li):
        even = li["even"]
        va = VAe if even else VA
        vt = [T("VA", t) for t in range(NT)]
        for cpos in ((64,) if even else (64, 256)):
            P.emit("dve", MSET(va[:, :, cpos:cpos + 64], 0.0), writes=vt)
            P.emit("dve", MSET(va[:, :, cpos:cpos + 1], 1.0), writes=vt)
        for blk in blocks_of():
            norm_block(b, l, 0, blk[3])
            in_proj_block(b, l, li, blk)
        if even:
            pool_mixer(b, l, li)
        if 'noattn' not in DBG:
            attention(b, l, li)
        if 'noout' not in DBG:
            out_proj(b, l, li)
        else:
            [w_pop('out', l, 0, s_) for s_ in range(4)]
            w_release(4)

    eps_t = sb("eps_t", [128, 1], F32)
    eps_ap = eps_t[:, 0:1]
    P.emit("dve", MSET(eps_t[:], EPS), writes=[c2])

    for b in range(nb):
        for t in range(NT):
            src = x_d[b, t * 128:(t + 1) * 128, :] if t < 16 else ctx_d[b, (t - 16) * 128:(t - 15) * 128, :]
            P.dma("sp", xs[:, t, :], src, s_x[t], writes=[T("x", t)])
        for l in layers:
            li = layer_info(l)
            mixer(b, l, li)
            if 'noffn' not in DBG:
                ffn(b, l, li)
        for t in range(16):
            P.dma("sp", out_d[b, t * 128:(t + 1) * 128, :], xs[:, t, :], s_out[t], reads=[T("x", t)])
    assert 'noffn' in DBG or (not sched and not inflight), (len(sched), len(inflight))

    P.play(final_waits=s_out)
    es.close()
    return nc, P.n_inst


def _rope_tables():
    quarter = 16
    freqs = (np.float32(10000.0) ** (-np.arange(quarter, dtype=np.float32) / np.float32(quarter))).astype(np.float32)
    t = np.arange(SEQ)
    rows = (t // 64).astype(np.float32)
    cols = (t % 64).astype(np.float32)
    cosT = np.zeros((128, SEQ), np.float32)
    sinT = np.zeros((128, SEQ), np.float32)
    for p in range(128):
        d = p % 64
        pos = rows if d < 32 else cols
        j = d % 16
        ang = (pos * freqs[j]).astype(np.float32)
        sign = -1.0 if (d % 32) < 16 else 1.0
        cosT[p] = np.cos(ang)
        sinT[p] = sign * np.sin(ang)
    return cosT, sinT


def _const_mats():
    cm = np.zeros((128, 5, 128), np.float32)
    cm[:, 0, :] = np.eye(128, dtype=np.float32)
    for m in range(128):
        d = m % 64
        partner = m + 16 if (d % 32) < 16 else m - 16
        cm[partner, 1, m] = 1.0
    for k in range(128):
        for m in range(128):
            if k // 64 == m // 64:
                cm[k, 2, m] = 1.0 / 64.0
    a = np.arange(128)[None, :]
    bb = np.arange(128)[:, None]
    cm[:, 3, :] = (a <= bb).astype(np.float32)
    cm[:, 4, :] = (a >= bb).astype(np.float32)
    S = 384
    mb = np.zeros((128, 4, 5, 128), np.float32)
    for gi, w in enumerate(WINS):
        Mm = np.zeros((S, S), np.float64)
        for t in range(S):
            lo = max(t - w // 2, 0)
            hi = min(t + w - w // 2, S)
            Mm[lo:hi, t] = 1.0 / (hi - lo)
            Mm[t, t] -= 1.0
        mb[:, gi, 0, :] = Mm[0:128, 0:128]
        mb[:, gi, 1, :] = Mm[128:256, 128:256]
        mb[:, gi, 2, :] = Mm[256:384, 256:384]
        mb[:, gi, 3, :] = Mm[0:128, 128:256]
        mb[:, gi, 4, :] = Mm[256:384, 128:256]
    return cm, mb


def _slots(W, ncol_slots):
    return np.ascontiguousarray(W.reshape(8, 128, ncol_slots, 256).transpose(2, 1, 0, 3))


def prepare_shared(inp):
    f = lambda a: np.asarray(a, dtype=np.float32)
    w_mod, b_mod = f(inp["w_mod"]), f(inp["b_mod"])
    sh = {}
    sh["wmod"] = np.ascontiguousarray(w_mod.reshape(DEPTH, 8, 128, 12, 512).transpose(0, 3, 2, 1, 4))
    sh["bmodT"] = np.ascontiguousarray(b_mod.reshape(DEPTH, 48, 128).transpose(2, 0, 1))
    gpre = np.stack([f(inp["g_pre_mix"]), f(inp["g_pre_ffn"])], axis=1)
    sh["gpreT"] = np.ascontiguousarray(gpre.reshape(DEPTH, 2, 8, 128).transpose(3, 0, 1, 2))
    sh["_gpost"] = np.stack([f(inp["g_post_mix"]), f(inp["g_post_ffn"])], axis=1)
    sh["_bgate"] = np.stack([b_mod[:, 2048:3072], b_mod[:, 5120:6144]], axis=1)
    we_in, we_out = f(inp["we_in"]), f(inp["we_out"])
    cols = []
    for c in range(4):
        cols += list(range(c * 64, (c + 1) * 64)) + list(range((c + 4) * 64, (c + 5) * 64))
    qperm_e = np.array(cols)
    colperm = np.concatenate([qperm_e, np.arange(512, 1280)])
    sh["wein"] = np.stack([_slots(we_in[i][:, colperm], 5) for i in range(2)])
    rowperm = np.concatenate([qperm_e, np.arange(512, 1024)])
    sh["weout"] = np.stack([_slots(we_out[i][rowperm, :], 4) for i in range(2)])
    sh["wepool"] = np.ascontiguousarray(f(inp["we_pool"]).transpose(2, 0, 1, 3))
    qg, kg = f(inp["we_q_gain"]), f(inp["we_k_gain"])
    sh["wegain"] = np.ascontiguousarray(np.stack([np.tile(qg, (1, 2)), np.tile(kg, (1, 2))], axis=2).transpose(1, 0, 2))
    sh["wepsc"] = np.ascontiguousarray(f(inp["we_pool_scale"]).reshape(2, 4, 128).transpose(2, 0, 1))
    wo_in, wo_out = f(inp["wo_in"]), f(inp["wo_out"])
    cols = []
    for c in range(8):
        k, r = c // 4, c % 4
        ha, hb = 8 * k + r, 8 * k + r + 4
        cols += list(range(ha * 64, (ha + 1) * 64)) + list(range(hb * 64, (hb + 1) * 64))
    qperm_o = np.array(cols)
    colperm = np.concatenate([qperm_o, np.arange(1024, 1536)])
    sh["woin"] = np.stack([_slots(wo_in[i][:, colperm], 6) for i in range(2)])
    sh["woout"] = np.stack([_slots(wo_out[i][qperm_o, :], 4) for i in range(2)])
    sh["wosink"] = np.ascontiguousarray(np.broadcast_to(f(inp["wo_sink"])[None], (128, 2, 16)))
    wfi, wfo = f(inp["w_ffn_in"]), f(inp["w_ffn_out"])
    g = wfi[:, :, :HID].reshape(DEPTH, 8, 128, NJ, 128)
    u = wfi[:, :, HID:].reshape(DEPTH, 8, 128, NJ, 128)
    sh["wfin"] = np.ascontiguousarray(np.concatenate([g, u], axis=4).transpose(0, 3, 2, 1, 4))
    sh["wfout"] = np.ascontiguousarray(wfo.reshape(DEPTH, NJ, 128, D).transpose(0, 2, 1, 3))
    cosT, sinT = _rope_tables()
    sh["cosT"], sh["sinT"] = cosT, sinT
    sh["cmat"], sh["mband"] = _const_mats()
    return sh


def core_inputs(sh, inp, b0, nb):
    f = lambda a: np.asarray(a, dtype=np.float32)
    NB1 = nb + 1
    m = {k: v for k, v in sh.items() if not k.startswith("_")}
    m["x"] = np.ascontiguousarray(f(inp["x"])[b0:b0 + nb])
    m["ctx"] = np.ascontiguousarray(f(inp["ctx"])[b0:b0 + nb])
    call = np.concatenate([f(inp["c"])[b0:b0 + nb], f(inp["c_ctx"])[None, :]], axis=0)
    m["scT"] = np.ascontiguousarray(call.reshape(NB1, 8, 128).transpose(2, 1, 0))
    m["bgate"] = np.ascontiguousarray(np.broadcast_to(sh["_bgate"][None], (NB1, DEPTH, 2, D)))
    m["gpost"] = np.ascontiguousarray(np.broadcast_to(sh["_gpost"][None], (NB1, DEPTH, 2, D)))
    return m


_CACHE = {}


def kernel(**inputs):
    nb = 32 // N_CORES
    key = ("full", nb)
    if key not in _CACHE:
        _CACHE[key] = build_program(nb, list(range(DEPTH)))[0]
    nc = _CACHE[key]
    sh = prepare_shared(inputs)
    in_maps = [core_inputs(sh, inputs, c * nb, nb) for c in range(N_CORES)]
    res = run_bass_kernel_spmd(nc, in_maps, core_ids=list(range(N_CORES)))
    return np.concatenate([np.asarray(r["out"], dtype=np.float32) for r in res.results], axis=0)
```

```python
import math
import os
DBG = set(os.environ.get('KDBG', '').split(','))
from contextlib import ExitStack
from collections import deque

import numpy as np
import concourse.bass as bass
import concourse.mybir as mybir
from concourse.bass_utils import run_bass_kernel_spmd

F32 = mybir.dt.float32
BF16 = mybir.dt.bfloat16
AF = mybir.ActivationFunctionType
ALU = mybir.AluOpType

D = 1024
SEQ = 2048
CTX = 256
DEPTH = 4
NT = 18
HID = 2816
NJ = 22
EPS = 1e-6
N_CORES = 8
SAME_SYNC = True
WINS = (2, 4, 8, 16)


class SemK:
    __slots__ = ("h", "total")

    def __init__(self, h):
        self.h = h
        self.total = 0


class Tr:
    __slots__ = ("w", "r")

    def __init__(self, base=None):
        self.w = dict(base) if base else {}
        self.r = {}


class Prog:
    ENG = ("pe", "act", "dve", "pool", "sp")

    def __init__(self, nc, es):
        self.nc = nc
        self.es = es
        self.streams = {e: [] for e in self.ENG}
        self.cnt = {e: 0 for e in self.ENG}
        self.waited = {e: {} for e in self.ENG}
        self.esem = {}
        for e in ("pe", "act", "dve", "pool"):
            self.esem[e] = self.new_sem("e_" + e)
        self.trs = {}
        self.ubase = {}
        self.ukeys = set()
        self.n_inst = 0

    def new_sem(self, name):
        return SemK(self.es.enter_context(self.nc.semaphore(name)))

    def T(self, *key):
        t = self.trs.get(key)
        if t is None:
            t = Tr(self.ubase if key[0] in self.ukeys else None)
            self.trs[key] = t
        return t

    def retire(self, names):
        for key in [k for k in self.trs if k[0] in names]:
            t = self.trs.pop(key)
            for d in (t.w, t.r):
                for s, v in d.items():
                    if self.ubase.get(s, 0) < v:
                        self.ubase[s] = v

    def emit(self, eng, fn, reads=(), writes=(), signal=True, dsem=None):
        deps = {}
        for t in reads:
            for s, v in t.w.items():
                if deps.get(s, 0) < v:
                    deps[s] = v
        for t in writes:
            for d in (t.w, t.r):
                for s, v in d.items():
                    if deps.get(s, 0) < v:
                        deps[s] = v
        own = self.esem.get(eng)
        wd = self.waited[eng]
        waits = []
        for s, v in deps.items():
            if s is own and (eng == "pe" or not SAME_SYNC):
                continue
            if wd.get(s, 0) >= v:
                continue
            wd[s] = v
            waits.append((s.h, v))
        if dsem is not None:
            dsem.total += 16
            tok_s, tok_v = dsem, dsem.total
            sig = False
        else:
            tok_s, tok_v = own, self.cnt[eng] + 1
            sig = signal
            if signal:
                self.cnt[eng] += 1
        for t in reads:
            if t.r.get(tok_s, 0) < tok_v:
                t.r[tok_s] = tok_v
        for t in writes:
            t.w = {tok_s: tok_v}
            t.r = {}
        self.streams[eng].append((waits, fn, sig))
        self.n_inst += 1 + len(waits)

    def dma(self, eng, out, in_, dsem, reads=(), writes=()):
        h = dsem.h
        self.emit(eng, lambda e: e.dma_start(out=out, in_=in_).then_inc(h, 16),
                  reads=reads, writes=writes, dsem=dsem)


    def dma_group(self, items, dsem, reads=(), writes=()):
        deps = {}
        for t in reads:
            for s, v in t.w.items():
                if deps.get(s, 0) < v:
                    deps[s] = v
        for t in writes:
            for d in (t.w, t.r):
                for s, v in d.items():
                    if deps.get(s, 0) < v:
                        deps[s] = v
        h = dsem.h
        for (eng, out, in_) in items:
            wd = self.waited[eng]
            waits = []
            for s, v in deps.items():
                if wd.get(s, 0) >= v:
                    continue
                wd[s] = v
                waits.append((s.h, v))
            dsem.total += 16
            self.streams[eng].append((waits, (lambda out=out, in_=in_: lambda e: e.dma_start(out=out, in_=in_).then_inc(h, 16))(), False))
            self.n_inst += 1 + len(waits)
        for t in reads:
            if t.r.get(dsem, 0) < dsem.total:
                t.r[dsem] = dsem.total
        for t in writes:
            t.w = {dsem: dsem.total}
            t.r = {}

    def play(self, final_waits):
        nc = self.nc
        with nc.Block() as block:
            def mk(ename):
                stream = self.streams[ename]
                own = self.esem.get(ename)

                def body(e):
                    for waits, fn, sig in stream:
                        for h, v in waits:
                            e.wait_ge(h, v)
                        ins = fn(e)
                        if sig:
                            ins.then_inc(own.h, 1)
                    if ename == "sp":
                        for s in final_waits:
                            if s.total:
                                e.wait_ge(s.h, s.total)
                return body
            block.tensor(mk("pe"))
            block.scalar(mk("act"))
            block.vector(mk("dve"))
            block.gpsimd(mk("pool"))
            block.sync(mk("sp"))


def MM(out, lhsT, rhs, st=True, sp=True):
    return lambda e: e.matmul(out, lhsT, rhs, start=st, stop=sp)


def TP(out, in_, ident):
    return lambda e: e.transpose(out, in_, ident)


def ACT(out, in_, func, **kw):
    return lambda e: e.activation(out=out, in_=in_, func=func, **kw)


def ACP(out, in_):
    return lambda e: e.copy(out=out, in_=in_)


def TT(out, a, b, op):
    return lambda e: e.tensor_tensor(out=out, in0=a, in1=b, op=op)


def TS(out, in0, s1, s2=None, op0=ALU.mult, op1=None):
    if op1 is None:
        return lambda e: e.tensor_scalar(out=out, in0=in0, scalar1=s1, scalar2=None, op0=op0)
    return lambda e: e.tensor_scalar(out=out, in0=in0, scalar1=s1, scalar2=s2, op0=op0, op1=op1)


def STT(out, in0, scalar, in1, op0, op1):
    return lambda e: e.scalar_tensor_tensor(out=out, in0=in0, scalar=scalar, in1=in1, op0=op0, op1=op1)


def RCP(out, in_):
    return lambda e: e.reciprocal(out=out, in_=in_)


def VCP(out, in_):
    return lambda e: e.tensor_copy(out=out, in_=in_)


def MSET(ap, v):
    return lambda e: e.memset(ap, v)


def layer_info(l):
    even = (l % 2 == 0)
    return dict(even=even, i=l // 2, nq=4 if even else 8, nk=1 if even else 2,
                nslot_in=5 if even else 6, with_ctx=(l < DEPTH - 1))


def weight_schedule(nb, layers):
    out = []
    for b in range(nb):
        for l in layers:
            li = layer_info(l)
            for n in range(5):
                for s in range(li["nslot_in"]):
                    want_q = (n < 4) or li["with_ctx"]
                    if not want_q and s < li["nq"] // 2:
                        continue
                    if not want_q and li["even"] and s >= 3:
                        continue
                    out.append(("in", l, n, s))
            for s in range(4):
                out.append(("out", l, 0, s))
            nblk = 5 if li["with_ctx"] else 4
            for n in range(nblk):
                for j in range(NJ):
                    out.append(("ffn", l, n, j))
    return out


def build_program(nb, layers):
    NB1 = nb + 1
    nc = bass.Bass("TRN2", target_bir_lowering=False)
    es = ExitStack()

    def din(name, shape, dt=F32):
        return nc.dram_tensor(name, list(shape), dt, kind="ExternalInput").ap()

    x_d = din("x", [nb, SEQ, D])
    ctx_d = din("ctx", [nb, CTX, D])
    scT_d = din("scT", [128, 8, NB1])
    wmod_d = din("wmod", [DEPTH, 12, 128, 8, 512])
    bmodT_d = din("bmodT", [128, DEPTH, 48])
    bgate_d = din("bgate", [NB1, DEPTH, 2, D])
    gpost_d = din("gpost", [NB1, DEPTH, 2, D])
    gpreT_d = din("gpreT", [128, DEPTH, 2, 8])
    wein_d = din("wein", [2, 5, 128, 8, 256])
    weout_d = din("weout", [2, 4, 128, 8, 256])
    wepool_d = din("wepool", [128, 2, 4, 128])
    wegain_d = din("wegain", [128, 2, 2])
    wepsc_d = din("wepsc", [128, 2, 4])
    woin_d = din("woin", [2, 6, 128, 8, 256])
    woout_d = din("woout", [2, 4, 128, 8, 256])
    wosink_d = din("wosink", [128, 2, 16])
    wfin_d = din("wfin", [DEPTH, NJ, 128, 8, 256])
    wfout_d = din("wfout", [DEPTH, 128, NJ, D])
    cos_d = din("cosT", [128, SEQ])
    sin_d = din("sinT", [128, SEQ])
    cmat_d = din("cmat", [128, 5, 128])
    mband_d = din("mband", [128, 4, 5, 128])
    out_d = nc.dram_tensor("out", [nb, SEQ, D], F32, kind="ExternalOutput").ap()
    gsc_d = nc.dram_tensor("gsc", [DEPTH, 2, NB1, D], F32, kind="Internal").ap()

    P = Prog(nc, es)
    T = P.T

    def sb(name, shape, dt):
        return es.enter_context(nc.sbuf_tensor(name, list(shape), dt))

    def ps(name, shape, dt):
        return es.enter_context(nc.psum_tensor(name, list(shape), dt))

    xs = sb("xs", [128, NT, D], F32)
    U = sb("U", [128, 16896], F32)
    hT = sb("hT", [128, 8, 512], BF16)
    NSLOT = 4
    wsl = sb("wsl", [128, NSLOT, 8, 256], BF16)
    cos_sb = sb("cos_sb", [128, SEQ], BF16)
    sin_sb = sb("sin_sb", [128, SEQ], BF16)
    GG = sb("GG", [128, 2, D], F32)
    xn2 = sb("xn2", [128, 2, D], BF16)
    tmp = sb("tmp", [128, D], F32)
    tmpb = tmp[:, :].bitcast(BF16)
    PT = sb("PT", [128, 3, 512], BF16)
    r_sd = sb("r_sd", [128, 512], F32)
    r_qn = sb("r_qn", [128, 512], BF16)
    r_t1 = sb("r_t1", [128, 512], F32)
    r_t2 = sb("r_t2", [128, 512], F32)
    r_sq = r_qn
    r_t2b = r_t2[:, :].bitcast(BF16)
    osb = r_t1
    rcp = r_sd
    cmat = sb("cmat_sb", [128, 5, 128], BF16)
    mband = sb("mband_sb", [128, 4, 5, 128], BF16)
    wpool = sb("wpool_sb", [128, 4, 128], BF16)
    onesel = sb("onesel", [128, 2, 128], BF16)
    stat = sb("stat", [128, 3, 4], F32)
    scT = sb("scT_sb", [128, 8, NB1], F32)
    AT = sb("AT", [128, DEPTH, 2, 8, NB1], F32)
    BT = sb("BT", [128, DEPTH, 2, 8, NB1], F32)
    wegain = sb("wegain_sb", [128, 2, 2], F32)
    wepsc = sb("wepsc_sb", [128, 2, 4], F32)
    esink = sb("esink", [128, 2, 16], F32)

    Ub = U[:, :].bitcast(BF16)
    QA = Ub[:, 0:18432].rearrange("p (c t) -> p c t", c=8)
    KT = Ub[:, 18432:23040].rearrange("p (c t) -> p c t", c=2)
    VA = Ub[:, 23040:29952].rearrange("p (t v) -> p t v", t=NT)
    KTe = Ub[:, 18432:20736].rearrange("p (c t) -> p c t", c=1)
    VAe = Ub[:, 20736:24192].rearrange("p (t v) -> p t v", t=NT)
    UT = Ub[:, 24192:33408].rearrange("p (t v) -> p t v", t=NT)
    w2 = Ub[:, 0:22528].rearrange("p (j n) -> p j n", j=NJ)
    mT = Ub[:, 22528:33792].rearrange("p (j n) -> p j n", j=NJ)
    wmst = U[:, 0:8192].rearrange("p (s k n) -> p s k n", s=2, k=8)
    grow = U[0:NB1, 8192:9728].rearrange("p (a n) -> p a n", a=3)
    modT = U[:, 9728:9728 + DEPTH * 48 * NB1].rearrange("p (l c b) -> p l c b", l=DEPTH, c=48)
    o_ = 9728 + DEPTH * 48 * NB1
    bmodT = U[:, o_:o_ + DEPTH * 48].rearrange("p (l c) -> p l c", l=DEPTH)
    gpreT = U[:, o_ + DEPTH * 48:o_ + DEPTH * 48 + DEPTH * 16].rearrange("p (l w c) -> p l w c", l=DEPTH, w=2)
    P.ukeys = {"QA", "KT", "VA", "UT", "w2", "mT", "wmst", "grow", "modT", "stc"}
    MIX_KEYS = {"QA", "KT", "VA", "UT"}
    FFN_KEYS = {"w2", "mT"}

    psT = ps("psT", [128, 8, 128], BF16)
    psA = ps("psA", [128, 512], F32)
    psB = ps("psB", [128, 1024], F32)
    psC = ps("psC", [128, 1024], F32)
    psD = ps("psD", [128, 1024], F32)
    bankB = [(psB[:, 0:512], ("psB", 0)), (psB[:, 512:1024], ("psB", 1))]
    bankC = [(psC[:, 0:512], ("psC", 0)), (psC[:, 512:1024], ("psC", 1))]
    bankS = bankB + [(psD[:, 0:512], ("psD", 0)), (psD[:, 512:1024], ("psD", 1))]

    s_const = P.new_sem("s_const")
    s_constb = P.new_sem("s_constb")
    s_x = [P.new_sem("s_x%d" % t) for t in range(NT)]
    s_out = [P.new_sem("s_o%d" % t) for t in range(16)]
    s_slot = [P.new_sem("s_ws%d" % s) for s in range(NSLOT)]
    s_w2 = P.new_sem("s_w2")
    s_wm = [P.new_sem("s_wm0"), P.new_sem("s_wm1")]
    s_gg = [P.new_sem("s_gg0"), P.new_sem("s_gg1")]
    s_gs = P.new_sem("s_gs")
    s_wp = P.new_sem("s_wp")
    s_stc = P.new_sem("s_stc")
    s_gr = [P.new_sem("s_gr1"), P.new_sem("s_gr2")]

    ident = cmat[:, 0, :]
    perm = cmat[:, 1, :]
    blkm = cmat[:, 2, :]
    maskL = cmat[:, 3, :]
    maskR = cmat[:, 4, :]

    cT = T("const")
    P.dma_group([
        ("pool", cmat[:], cmat_d[:, :, :]),
        ("pool", mband[:], mband_d[:, :, :, :]),
        ("pool", cos_sb[:], cos_d[:, :]),
        ("pool", sin_sb[:], sin_d[:, :]),
    ], s_const, writes=[cT])
    cTb = T("constb")
    P.dma_group([
        ("sp", scT[:], scT_d[:, :, :]),
        ("sp", wegain[:], wegain_d[:, :, :]),
        ("sp", wepsc[:], wepsc_d[:, :, :]),
        ("sp", esink[:], wosink_d[:, :, :]),
    ], s_constb, writes=[cTb])
    P.emit("act", ACT(scT[:], scT[:], AF.Silu), reads=[cTb], writes=[cTb])
    P.emit("act", ACT(esink[:], esink[:], AF.Exp), reads=[cTb], writes=[cTb])
    P.emit("act", ACP(stat[:, 0, 0:1], esink[:, 0, 0:1]), reads=[cT, cTb], writes=[cT])
    P.dma_group([("sp", bmodT, bmodT_d[:, :, :]), ("sp", gpreT, gpreT_d[:, :, :, :])], s_stc, writes=[T("stc")])
    c2 = T("const2")
    P.emit("dve", MSET(onesel[:], 0.0), writes=[c2])
    P.emit("dve", MSET(onesel[:, 0, 0:64], 1.0), writes=[c2])
    P.emit("dve", MSET(onesel[:, 1, 64:128], 1.0), writes=[c2])

    GATE_CB = {4: (0, 0), 5: (0, 1), 10: (1, 0), 11: (1, 1)}
    cnt_wm = 0
    for l in layers:
        for cb in range(12):
            sl = cnt_wm % 2
            cnt_wm += 1
            q = "sp" if cnt_wm % 2 else "act"
            P.dma(q, wmst[:, sl], wmod_d[l, cb], s_wm[sl], writes=[T("wmst", sl)])
            if cb in GATE_CB:
                which, half = GATE_CB[cb]
                hs = slice(half * 512, (half + 1) * 512)
                for kc in range(8):
                    P.emit("pe", MM(psB[0:NB1, 0:512], scT[:, kc, :], wmst[:, sl, kc, :], kc == 0, kc == 7),
                           reads=[cT, T("wmst", sl)], writes=[T("psB", 0)], signal=(kc == 7))
                P.dma("sp", grow[:, 1, :], bgate_d[:, l, which, hs], s_gr[0], writes=[T("grow", 1)])
                P.dma("sp", grow[:, 2, :], gpost_d[:, l, which, hs], s_gr[1], writes=[T("grow", 2)])
                P.emit("dve", TT(grow[:, 0, :], psB[0:NB1, 0:512], grow[:, 1, :], ALU.add),
                       reads=[T("psB", 0), T("grow", 1)], writes=[T("grow", 0)])
                P.emit("dve", TT(grow[:, 0, :], grow[:, 0, :], grow[:, 2, :], ALU.mult),
                       reads=[T("grow", 2)], writes=[T("grow", 0)])
                P.dma("sp", gsc_d[l, which, :, hs], grow[:, 0, :], s_gs, reads=[T("grow", 0)], writes=[T("gsc")])
            else:
                for fc in range(4):
                    ch = cb * 4 + fc
                    pa = psA[:, fc * NB1:(fc + 1) * NB1]
                    for kc in range(8):
                        P.emit("pe", MM(pa, wmst[:, sl, kc, fc * 128:(fc + 1) * 128], scT[:, kc, :], kc == 0, kc == 7),
                               reads=[cT, T("wmst", sl)], writes=[T("psA")], signal=(kc == 7))
                    P.emit("dve", TS(modT[:, l, ch, :], pa, bmodT[:, l, ch:ch + 1], op0=ALU.add),
                           reads=[T("psA"), T("stc")], writes=[T("modT")])
        for which, sc0 in ((0, 8), (1, 32)):
            for c in range(8):
                P.emit("dve", TS(AT[:, l, which, c, :], modT[:, l, sc0 + c, :], 1.0, gpreT[:, l, which, c:c + 1],
                                 op0=ALU.add, op1=ALU.mult),
                       reads=[T("modT"), T("stc")], writes=[T("AT")])
            sh0_ = 0 if which == 0 else 24
            P.emit("dve", VCP(BT[:, l, which, :, :], modT[:, l, sh0_:sh0_ + 8, :]), reads=[T("modT")], writes=[T("AT")])
    P.retire({"wmst", "grow", "modT", "stc"})

    sched = deque(weight_schedule(nb, layers))
    inflight = deque()
    ring = {"issued": 0, "released": 0, "popped": 0}

    def w_src(key):
        kind, l, n, s = key
        li = layer_info(l)
        if kind == "in":
            return (wein_d if li["even"] else woin_d)[li["i"], s]
        if kind == "out":
            return (weout_d if li["even"] else woout_d)[li["i"], s]
        return wfin_d[l, s]

    def w_prefetch():
        while sched and ring["issued"] - ring["released"] < NSLOT:
            key = sched.popleft()
            s = ring["issued"] % NSLOT
            ring["issued"] += 1
            P.dma("pool", wsl[:, s], w_src(key), s_slot[s], writes=[T("wsl", s)])
            inflight.append((key, s))

    def w_pop(kind, l, n, s):
        w_prefetch()
        key, slot = inflight.popleft()
        assert key == (kind, l, n, s), (key, (kind, l, n, s))
        ring["popped"] += 1
        return slot

    def w_release(k=1):
        ring["released"] += k
        assert ring["released"] <= ring["popped"]
        w_prefetch()

    ev = {"i": 0}

    def blocks_of():
        bl = [(n, n * 512, 512, list(range(4 * n, 4 * n + 4))) for n in range(4)]
        bl.append((4, SEQ, 256, [16, 17]))
        return bl

    hT2 = GG[:, :, :].rearrange("p a n -> p (a n)").bitcast(BF16).rearrange("p (c t) -> p c t", c=8)
    HB = [(hT, "hT"), (hT2, "hT2")]
    hcur = {"b": HB[0]}
    xn_state = {"i": 0}

    def norm_pre(b, l, which, t):
        col = ev["i"] % 4
        ev["i"] += 1
        xi = xn_state["i"] % 2
        xn_state["i"] += 1
        xb = xn2[:, xi, :]
        xt = T("x", t)
        sT_ = T("stat", col)
        P.emit("act", ACT(xb, xs[:, t, :], AF.Square, accum_out=stat[:, 0, col:col + 1]),
               reads=[xt], writes=[sT_, T("xn", xi)])
        P.emit("act", ACT(stat[:, 1, col:col + 1], stat[:, 0, col:col + 1], AF.Sqrt, scale=1.0 / D, bias=eps_ap),
               reads=[c2], writes=[sT_])
        P.emit("dve", RCP(stat[:, 2, col:col + 1], stat[:, 1, col:col + 1]), writes=[sT_])
        P.emit("dve", TS(xb, xs[:, t, :], stat[:, 2, col:col + 1]), reads=[xt, sT_], writes=[T("xn", xi)])
        return xi

    def norm_post(b, l, which, t, k, xi, hb=None):
        hb = hb or HB[0]
        bsel = nb if t >= 16 else b
        for c in range(8):
            P.emit("pe", TP(psT[:, c, :], xn2[:, xi, c * 128:(c + 1) * 128], ident),
                   reads=[T("xn", xi), cT], writes=[T("psT")], signal=(c == 7))
        for c in range(8):
            a_ap = AT[:, l, which, c, bsel:bsel + 1]
            b_ap = BT[:, l, which, c, bsel:bsel + 1]
            dst = hb[0][:, c, k * 128:(k + 1) * 128]
            wr = [T(hb[1], k)]
            if hb[1] == "hT2" and k == 0 and c == 0:
                wr = wr + [T("GG", 0), T("GG", 1)]
            if c % 2 == 0:
                P.emit("act", ACT(dst, psT[:, c, :], AF.Identity, scale=a_ap, bias=b_ap),
                       reads=[T("psT"), T("AT")], writes=wr)
            else:
                P.emit("dve", TS(dst, psT[:, c, :], a_ap, b_ap, op0=ALU.mult, op1=ALU.add),
                       reads=[T("psT"), T("AT")], writes=wr)

    def norm_block(b, l, which, tiles, hb=None):
        xi = norm_pre(b, l, which, tiles[0])
        for k, t in enumerate(tiles):
            nxt = norm_pre(b, l, which, tiles[k + 1]) if k + 1 < len(tiles) else None
            norm_post(b, l, which, t, k, xi, hb)
            xi = nxt

    def norm_thunks(b, l, which, tiles, hb):
        st_ = {}
        th = []

        def first():
            st_["xi"] = norm_pre(b, l, which, tiles[0])
        th.append(first)
        for k, t in enumerate(tiles):
            def step(k=k, t=t):
                nxt = norm_pre(b, l, which, tiles[k + 1]) if k + 1 < len(tiles) else None
                norm_post(b, l, which, t, k, st_["xi"], hb)
                st_["xi"] = nxt
            th.append(step)
        return deque(th)

    pp_state = {"i": 0}

    def proj_fm(slot, off, N, ntl):
        ap_, key = bankB[pp_state["i"] % 2]
        pp_state["i"] += 1
        for kc in range(8):
            P.emit("pe", MM(ap_[:, 0:N], wsl[:, slot, kc, off:off + 128], hcur["b"][0][:, kc, 0:N], kc == 0, kc == 7),
                   reads=[T("wsl", slot)] + [T(hcur["b"][1], k) for k in range(ntl)], writes=[T(*key)], signal=(kc == 7))
        if hcur.get("extras"):
            hcur["extras"].popleft()()
        return ap_, key

    def qk_post(li, pp, pkey, N, s0, is_ctx, gain_ap, dst, dst_tr):
        ppt = T(*pkey)
        if 'noqk' in DBG:
            return
        cs = cos_sb[:, s0:s0 + N] if not is_ctx else None
        sn = sin_sb[:, s0:s0 + N] if not is_ctx else None
        if li["even"]:
            sqb = PT[:, 0, :]
            P.emit("act", ACT(sqb[:, 0:N], pp[:, 0:N], AF.Square), reads=[ppt], writes=[T("PT", 0)])
            P.emit("dve", TS(r_qn[:, 0:N], pp[:, 0:N], gain_ap), reads=[ppt, cT, T("PT", 0)], writes=[T("r_qn")])
            P.emit("pe", MM(psC[:, 0:N], blkm, sqb[:, 0:N]), reads=[T("PT", 0), cT], writes=[T("psC", 0)])
            if not is_ctx:
                P.emit("pe", MM(psC[:, 512:512 + N], perm, r_qn[:, 0:N]), reads=[T("r_qn"), cT], writes=[T("psC", 1)])
            P.emit("act", ACT(r_sd[:, 0:N], psC[:, 0:N], AF.Sqrt, bias=eps_ap, scale=1.0),
                   reads=[T("psC", 0), c2], writes=[T("r_sd")])
            P.emit("dve", RCP(r_sd[:, 0:N], r_sd[:, 0:N]), writes=[T("r_sd")])
            if is_ctx:
                P.emit("dve", TT(dst, r_qn[:, 0:N], r_sd[:, 0:N], ALU.mult), reads=[T("r_qn"), T("r_sd")], writes=[dst_tr])
                return
            P.emit("dve", TT(r_t1[:, 0:N], r_qn[:, 0:N], cs, ALU.mult), reads=[T("r_qn"), cT], writes=[T("r_t1")])
            P.emit("dve", TT(r_t2[:, 0:N], psC[:, 512:512 + N], sn, ALU.mult), reads=[T("psC", 1), cT], writes=[T("r_t2")])
            P.emit("dve", TT(r_t1[:, 0:N], r_t1[:, 0:N], r_t2[:, 0:N], ALU.add), reads=[T("r_t2")], writes=[T("r_t1")])
            P.emit("dve", TT(dst, r_t1[:, 0:N], r_sd[:, 0:N], ALU.mult), reads=[T("r_t1"), T("r_sd")], writes=[dst_tr])
            return
        else:
            if is_ctx:
                P.emit("act", ACP(dst, pp[:, 0:N]), reads=[ppt], writes=[dst_tr])
                return
            P.emit("act", ACP(r_qn[:, 0:N], pp[:, 0:N]), reads=[ppt], writes=[T("r_qn")])
            P.emit("pe", MM(psC[:, 512:512 + N], perm, r_qn[:, 0:N]), reads=[T("r_qn"), cT], writes=[T("psC", 1)])
            P.emit("dve", TT(r_t1[:, 0:N], r_qn[:, 0:N], cs, ALU.mult), reads=[T("r_qn"), cT], writes=[T("r_t1")])
        P.emit("dve", TT(r_t2[:, 0:N], psC[:, 512:512 + N], sn, ALU.mult), reads=[T("psC", 1), cT], writes=[T("r_t2")])
        P.emit("dve", TT(dst, r_t1[:, 0:N], r_t2[:, 0:N], ALU.add), reads=[T("r_t1"), T("r_t2")], writes=[dst_tr])

    def in_proj_block(b, l, li, blk):
        n, s0, N, tiles = blk
        is_ctx = (n == 4)
        ntl = len(tiles)
        want_q = (not is_ctx) or li["with_ctx"]
        even = li["even"]
        i = li["i"]
        kt = KTe if even else KT
        va = VAe if even else VA
        nqs = li["nq"] // 2
        st = {"vslot": None, "voff": 0, "VW": 0}

        def chunk_iter():
            if want_q:
                for s in range(nqs):
                    slot = w_pop("in", l, n, s)
                    for hf in range(2):
                        c = 2 * s + hf
                        yield (slot, hf * 128, hf == 1,
                               (wegain[:, i, 0:1] if even else None, QA[:, c, s0:s0 + N], T("QA", c, n)))
            if even:
                st["vslot"] = w_pop("in", l, n, 2)
                st["voff"], st["VW"] = 128, 128
                yield (st["vslot"], 0, False, (wegain[:, i, 1:2], kt[:, 0, s0:s0 + N], T("KT", 0, n)))
            else:
                slot = w_pop("in", l, n, 4)
                for hf in range(2):
                    yield (slot, hf * 128, hf == 1, (None, kt[:, hf, s0:s0 + N], T("KT", hf, n)))

        prev = None
        for (slot, off, rel, pa) in chunk_iter():
            pp, pkey = proj_fm(slot, off, N, ntl)
            if rel:
                w_release()
            if prev is not None:
                qk_post(*prev)
            prev = (li, pp, pkey, N, s0, is_ctx) + pa
        if not even:
            st["vslot"] = w_pop("in", l, n, 5)
            st["voff"], st["VW"] = 0, 256
        vslot, voff, VW = st["vslot"], st["voff"], st["VW"]
        for k, t in enumerate(tiles):
            if k == 1 and prev is not None:
                qk_post(*prev)
                prev = None
            for kc in range(8):
                P.emit("pe", MM(psA[:, 0:VW], hcur["b"][0][:, kc, k * 128:(k + 1) * 128], wsl[:, vslot, kc, voff:voff + VW],
                                kc == 0, kc == 7),
                       reads=[T("wsl", vslot), T(hcur["b"][1], k)], writes=[T("psA")], signal=(kc == 7))
            if even:
                P.emit("act", ACP(va[:, t, 0:64], psA[:, 0:64]), reads=[T("psA")], writes=[T("VA", t)])
                P.emit("dve", VCP(va[:, t, 128:192], psA[:, 64:128]), reads=[T("psA")], writes=[T("VA", t)])
            else:
                for gi, vb_ in enumerate((0, 128, 192, 320)):
                    if gi % 2 == 0:
                        P.emit("act", ACP(va[:, t, vb_:vb_ + 64], psA[:, gi * 64:(gi + 1) * 64]), reads=[T("psA")], writes=[T("VA", t)])
                    else:
                        P.emit("dve", VCP(va[:, t, vb_:vb_ + 64], psA[:, gi * 64:(gi + 1) * 64]), reads=[T("psA")], writes=[T("VA", t)])
        if prev is not None:
            qk_post(*prev)
            prev = None
        w_release()
        if even and want_q:
            s3 = w_pop("in", l, n, 3)
            s4 = w_pop("in", l, n, 4)
            for k, t in enumerate(tiles):
                ap_, key = bankB[pp_state["i"] % 2]
                pp_state["i"] += 1
                for hs, slot in enumerate((s3, s4)):
                    for kc in range(8):
                        P.emit("pe", MM(ap_[:, hs * 256:(hs + 1) * 256], hcur["b"][0][:, kc, k * 128:(k + 1) * 128],
                                        wsl[:, slot, kc, :], kc == 0, kc == 7),
                               reads=[T("wsl", slot), T(hcur["b"][1], k)], writes=[T(*key)], signal=(kc == 7))
                if k % 2 == 0:
                    P.emit("act", ACP(UT[:, t, :], ap_[:, 0:512]), reads=[T(*key)], writes=[T("UT", t)])
                else:
                    P.emit("dve", VCP(UT[:, t, :], ap_[:, 0:512]), reads=[T(*key)], writes=[T("UT", t)])
            w_release(2)

    def pool_mixer(b, l, li):
        i = li["i"]
        P.dma("pool", wpool[:], wepool_d[:, i, :, :], s_wp, writes=[T("wpool")])
        for g in range(4):
            for blk in blocks_of():
                n, s0, N, tiles = blk
                if n == 4 and not li["with_ctx"]:
                    continue
                first, last = (16, 17) if n == 4 else (0, 15)
                ap_, key = bankB[pp_state["i"] % 2]
                pp_state["i"] += 1
                for k, t in enumerate(tiles):
                    contrib = []
                    if t > first:
                        contrib.append((t - 1, 3))
                    contrib.append((t, 0 if t == first else (2 if t == last else 1)))
                    if t < last:
                        contrib.append((t + 1, 4))
                    for ci, (s, kind) in enumerate(contrib):
                        lastmm = (ci == len(contrib) - 1)
                        P.emit("pe", MM(ap_[:, k * 128:(k + 1) * 128], UT[:, s, g * 128:(g + 1) * 128],
                                        mband[:, g, kind, :], ci == 0, lastmm),
                               reads=[T("UT", s), cT], writes=[T(*key)], signal=(lastmm and k == len(tiles) - 1))
                P.emit("act", ACP(r_qn[:, 0:N], ap_[:, 0:N]), reads=[T(*key)], writes=[T("r_qn")])
                P.emit("pe", MM(psC[:, 0:N], wpool[:, g, :], r_qn[:, 0:N]), reads=[T("r_qn"), T("wpool")], writes=[T("psC", 0)])
                P.emit("dve", TS(QA[:, 4 + g, s0:s0 + N], psC[:, 0:N], wepsc[:, i, g:g + 1]),
                       reads=[T("psC", 0), cT], writes=[T("QA", 4 + g, n)])

    at_state = {"s": 0, "o": 0, "pt": 0}

    def attention(b, l, li):
        even = li["even"]
        i = li["i"]
        kt = KTe if even else KT
        va = VAe if even else VA
        units = []
        for c in range(li["nq"]):
            for p in range(2):
                for blk in blocks_of():
                    if blk[0] == 4 and not li["with_ctx"]:
                        continue
                    units.append((c, p, blk))

        def unit_setup(c, p, blk):
            n, s0, N, tiles = blk
            u = dict(c=c, p=p, n=n, s0=s0, N=N, p0=p * 64)
            if even:
                u["kch"], g, u["head"] = 0, p, c + 4 * p
                u["vbase"] = 0 if g == 0 else 64
            else:
                kk, r = c // 4, c % 4
                g = 2 * kk + p
                u["head"] = 8 * kk + r + 4 * p
                u["kch"] = kk
                u["vbase"] = (0, 64, 192, 256)[g]
            u["M"] = 128
            u["dp"] = 64 if p == 0 else 0
            if n == 4:
                kl = [(16, 0, N, None), (17, 0, N, None)]
            elif even:
                kl = [(j, 0, N, None) for j in range(NT)]
            else:
                kl = [(16, 0, N, None), (17, 0, N, None)]
                t0 = 4 * n
                for j in range(max(t0 - 1, 0), min(t0 + 4, 15) + 1):
                    lo = max(j - 1, t0)
                    hi = min(j + 1, t0 + 3)
                    masks = []
                    if j - 1 >= t0:
                        masks.append((j - 1 - lo, maskR))
                    if j + 1 <= t0 + 3:
                        masks.append((j + 1 - lo, maskL))
                    kl.append((j, (lo - t0) * 128, (hi - t0 + 1) * 128, masks))
            u["kl"] = kl
            u["po"], u["pokey"] = bankC[at_state["o"] % 2]
            at_state["o"] += 1
            u["qtr"] = T("QA", c, n, p)
            return u

        def emit_S(u, idx):
            j, qa, qb, masks = u["kl"][idx]
            c, n, p0, s0, kch = u["c"], u["n"], u["p0"], u["s0"], u["kch"]
            pS, pSkey = bankS[at_state["s"] % 4]
            at_state["s"] += 1
            pti = at_state["pt"] % 3
            at_state["pt"] += 1
            W = qb - qa
            P.emit("pe", MM(pS[:, 0:W], kt[p0:p0 + 64, kch, j * 128:(j + 1) * 128],
                            QA[p0:p0 + 64, c, s0 + qa:s0 + qb]),
                   reads=[T("KT", kch, 4 if j >= 16 else j // 4), T("QA", c, n), u["qtr"]], writes=[T(*pSkey)])
            P.emit("act", ACT(PT[:, pti, 0:W], pS[:, 0:W], AF.Exp, scale=0.125),
                   reads=[T(*pSkey)], writes=[T("PT", pti)])
            for (mt, mk) in (masks or []):
                P.emit("dve", TT(PT[:, pti, mt * 128:(mt + 1) * 128], PT[:, pti, mt * 128:(mt + 1) * 128], mk, ALU.mult),
                       reads=[cT], writes=[T("PT", pti)])
            return pti

        def emit_PV(u, idx, pti):
            j, qa, qb, masks = u["kl"][idx]
            W = qb - qa
            lastk = (idx == len(u["kl"]) - 1)
            M, vbase = u["M"], u["vbase"]
            P.emit("pe", MM(u["po"][0:M, qa:qb], va[:, j, vbase:vbase + M], PT[:, pti, 0:W], idx == 0, lastk),
                   reads=[T("VA", j), T("PT", pti)], writes=[T(*u["pokey"])], signal=lastk)

        def emit_norm(u):
            po, pokey, N, dp, p0, p, c, s0 = u["po"], u["pokey"], u["N"], u["dp"], u["p0"], u["p"], u["c"], u["s0"]
            if even:
                P.emit("dve", RCP(rcp[dp:dp + 1, 0:N], po[dp:dp + 1, 0:N]), reads=[T(*pokey)], writes=[T("r_sd")])
            else:
                P.emit("dve", TS(rcp[dp:dp + 1, 0:N], po[dp:dp + 1, 0:N], esink[dp:dp + 1, i, u["head"]:u["head"] + 1], op0=ALU.add),
                       reads=[T(*pokey), cT], writes=[T("r_sd")])
                P.emit("dve", RCP(rcp[dp:dp + 1, 0:N], rcp[dp:dp + 1, 0:N]), writes=[T("r_sd")])
            rhi = r_qn[dp:dp + 1, 0:N]
            rlo = r_t2b[dp:dp + 1, 0:N]
            P.emit("dve", VCP(rhi, rcp[dp:dp + 1, 0:N]), reads=[T("r_sd")], writes=[T("r_qn")])
            P.emit("dve", TT(rlo, rcp[dp:dp + 1, 0:N], rhi, ALU.subtract), reads=[T("r_sd"), T("r_qn")], writes=[T("r_t2")])
            P.emit("pe", MM(psA[:, 0:N], onesel[dp:dp + 1, p, :], rhi, True, False),
                   reads=[T("r_qn"), c2], writes=[T("psA")], signal=False)
            P.emit("pe", MM(psA[:, 0:N], onesel[dp:dp + 1, p, :], rlo, False, True),
                   reads=[T("r_t2"), c2], writes=[T("psA")])
            P.emit("act", ACP(osb[p0:p0 + 64, 0:N], po[p0:p0 + 64, 0:N]), reads=[T(*pokey)], writes=[T("r_t1")])
            P.emit("dve", TT(QA[p0:p0 + 64, c, s0:s0 + N], osb[p0:p0 + 64, 0:N], psA[p0:p0 + 64, 0:N], ALU.mult),
                   reads=[T("r_t1"), T("psA")], writes=[u["qtr"]])

        prev_u = None
        for (c, p, blk) in units:
            u = unit_setup(c, p, blk)
            nk = len(u["kl"])
            pts = [emit_S(u, ii) for ii in range(min(2, nk))]
            for idx in range(nk):
                if idx + 2 < nk:
                    pts.append(emit_S(u, idx + 2))
                emit_PV(u, idx, pts[idx])
                if idx == min(1, nk - 1) and prev_u is not None:
                    emit_norm(prev_u)
            prev_u = u
        if prev_u is not None:
            emit_norm(prev_u)

    gg_state = {"i": 0}

    def load_gg(l, which, bsel):
        k = gg_state["i"] % 2
        gg_state["i"] += 1
        P.dma("sp", GG[:, k, :], gsc_d[l, which, bsel, :].partition_broadcast(128), s_gg[k],
              reads=[T("gsc")], writes=[T("GG", k)] + [T("hT2", kk) for kk in range(4)])
        return k

    y_state = {"i": 0}

    def resid(t, py, pykeys, ggk):
        col = ev["i"] % 4
        ev["i"] += 1
        sT_ = T("stat", col)
        ptr = [T(*k) for k in pykeys]
        P.emit("act", ACT(tmpb[:, 0:D], py[:, 0:D], AF.Square, accum_out=stat[:, 0, col:col + 1]), reads=ptr, writes=[sT_, T("tmp")])
        P.emit("act", ACT(stat[:, 1, col:col + 1], stat[:, 0, col:col + 1], AF.Sqrt, scale=1.0 / D, bias=eps_ap), reads=[c2], writes=[sT_])
        P.emit("dve", RCP(stat[:, 2, col:col + 1], stat[:, 1, col:col + 1]), writes=[sT_])
        P.emit("dve", STT(tmp[:], py[:, 0:D], stat[:, 2, col:col + 1], GG[:, ggk, :], ALU.mult, ALU.mult),
               reads=ptr + [sT_, T("GG", ggk)], writes=[T("tmp")])
        P.emit("dve", TT(xs[:, t, :], xs[:, t, :], tmp[:], ALU.add), reads=[T("tmp")], writes=[T("x", t)])

    def y_banks():
        k = y_state["i"] % 2
        y_state["i"] += 1
        if k == 0:
            return psD, [("psD", 0), ("psD", 1)]
        return psC, [("psC", 0), ("psC", 1)]

    def out_proj(b, l, li):
        slots = [w_pop("out", l, 0, s) for s in range(4)]
        ntile = NT if li["with_ctx"] else 16
        gk_lat = load_gg(l, 0, b)
        gk_ctx = load_gg(l, 0, nb) if li["with_ctx"] else None
        for t in range(ntile):
            py, pykeys = y_banks()
            n = 4 if t >= 16 else t // 4
            for q4 in range(4):
                for kc in range(8):
                    rd = [T("wsl", slots[q4]), T("QA", kc, n)]
                    if kc < li["nq"]:
                        rd += [T("QA", kc, n, 0), T("QA", kc, n, 1)]
                    P.emit("pe", MM(py[:, q4 * 256:(q4 + 1) * 256], QA[:, kc, t * 128:(t + 1) * 128],
                                    wsl[:, slots[q4], kc, :], kc == 0, kc == 7),
                           reads=rd, writes=[T(*pykeys[q4 // 2])], signal=(kc == 7))
            resid(t, py, pykeys, gk_ctx if t >= 16 else gk_lat)
        w_release(4)

    gu_state = {"i": 0}

    def ffn(b, l, li):
        P.retire(MIX_KEYS)
        P.dma_group([("pool", w2[:, j, :], wfout_d[l, :, j, :]) for j in range(NJ)], s_w2,
                    writes=[T("w2", j) for j in range(NJ)])
        gk_lat = load_gg(l, 1, b)
        gk_ctx = load_gg(l, 1, nb) if li["with_ctx"] else None
        fblocks = [blk for blk in blocks_of() if not (blk[0] == 4 and not li["with_ctx"])]
        norm_block(b, l, 1, fblocks[0][3])
        for bi, blk in enumerate(fblocks):
            n, s0, N, tiles = blk
            ntl = len(tiles)
            ntiles_next = fblocks[bi + 1][3] if bi + 1 < len(fblocks) else []
            for j in range(NJ):
                slot = w_pop("ffn", l, n, j)
                k = gu_state["i"] % 2
                gu_state["i"] += 1
                (pg, pgk), (pu, puk) = (bankB if k == 0 else bankC)
                for (pa, pk, off) in ((pg, pgk, 0), (pu, puk, 128)):
                    for kc in range(8):
                        P.emit("pe", MM(pa[:, 0:N], wsl[:, slot, kc, off:off + 128], hT[:, kc, 0:N], kc == 0, kc == 7),
                               reads=[T("wsl", slot)] + [T("hT", kk) for kk in range(ntl)], writes=[T(*pk)], signal=(kc == 7))
                w_release()
                sg, sgk = (r_t1, "r_t1") if k == 0 else (r_t2, "r_t2")
                P.emit("act", ACT(sg[:, 0:N], pg[:, 0:N], AF.Silu), reads=[T(*pgk)], writes=[T(sgk)])
                P.emit("dve", TT(mT[:, j, 0:N], sg[:, 0:N], pu[:, 0:N], ALU.mult), reads=[T(sgk), T(*puk)], writes=[T("mT", j)])
            for k, t in enumerate(tiles):
                xi = norm_pre(b, l, 1, ntiles_next[k]) if k < len(ntiles_next) else None
                py, pykeys = y_banks()
                for f in range(2):
                    for j in range(NJ):
                        P.emit("pe", MM(py[:, f * 512:(f + 1) * 512], mT[:, j, k * 128:(k + 1) * 128],
                                        w2[:, j, f * 512:(f + 1) * 512], j == 0, j == NJ - 1),
                               reads=[T("mT", j), T("w2", j)], writes=[T(*pykeys[f])], signal=(j == NJ - 1))
                resid(t, py, pykeys, gk_ctx if t >= 16 else gk_lat)
                if xi is not None:
                    norm_post(b, l, 1, ntiles_next[k], k, xi)
        P.retire(FFN_KEYS)

    def mixer(b, l, li):
        even = li["even"]
        va = VAe if even else VA
        vt = [T("VA", t) for t in range(NT)]
        for cpos in ((64,) if even else (64, 256)):
            P.emit("dve", MSET(va[:, :, cpos:cpos + 64], 0.0), writes=vt)
            P.emit("dve", MSET(va[:, :, cpos:cpos + 1], 1.0), writes=vt)
        mblocks = blocks_of()
        norm_block(b, l, 0, mblocks[0][3], HB[0])
        for bi, blk in enumerate(mblocks):
            hcur["b"] = HB[bi % 2]
            hcur["extras"] = (norm_thunks(b, l, 0, mblocks[bi + 1][3], HB[(bi + 1) % 2])
                              if bi + 1 < len(mblocks) else None)
            in_proj_block(b, l, li, blk)
            while hcur["extras"]:
                hcur["extras"].popleft()()
        hcur["b"] = HB[0]
        hcur["extras"] = None
        if even:
            pool_mixer(b, l, li)
        if 'noattn' not in DBG:
            attention(b, l, li)
        if 'noout' not in DBG:
            out_proj(b, l, li)
        else:
            [w_pop('out', l, 0, s_) for s_ in range(4)]
            w_release(4)

    eps_t = sb("eps_t", [128, 1], F32)
    eps_ap = eps_t[:, 0:1]
    P.emit("dve", MSET(eps_t[:], EPS), writes=[c2])

    for b in range(nb):
        for t in range(NT):
            src = x_d[b, t * 128:(t + 1) * 128, :] if t < 16 else ctx_d[b, (t - 16) * 128:(t - 15) * 128, :]
            P.dma("sp", xs[:, t, :], src, s_x[t], writes=[T("x", t)])
        for l in layers:
            li = layer_info(l)
            mixer(b, l, li)
            if 'noffn' not in DBG:
                ffn(b, l, li)
        for t in range(16):
            P.dma("sp", out_d[b, t * 128:(t + 1) * 128, :], xs[:, t, :], s_out[t], reads=[T("x", t)])
    assert 'noffn' in DBG or (not sched and not inflight), (len(sched), len(inflight))

    P.play(final_waits=s_out)
    es.close()
    return nc, P.n_inst


def _rope_tables():
    quarter = 16
    freqs = (np.float32(10000.0) ** (-np.arange(quarter, dtype=np.float32) / np.float32(quarter))).astype(np.float32)
    t = np.arange(SEQ)
    rows = (t // 64).astype(np.float32)
    cols = (t % 64).astype(np.float32)
    cosT = np.zeros((128, SEQ), np.float32)
    sinT = np.zeros((128, SEQ), np.float32)
    for p in range(128):
        d = p % 64
        pos = rows if d < 32 else cols
        j = d % 16
        ang = (pos * freqs[j]).astype(np.float32)
        sign = -1.0 if (d % 32) < 16 else 1.0
        cosT[p] = np.cos(ang)
        sinT[p] = sign * np.sin(ang)
    return cosT, sinT


def _const_mats():
    cm = np.zeros((128, 5, 128), np.float32)
    cm[:, 0, :] = np.eye(128, dtype=np.float32)
    for m in range(128):
        d = m % 64
        partner = m + 16 if (d % 32) < 16 else m - 16
        cm[partner, 1, m] = 1.0
    for k in range(128):
        for m in range(128):
            if k // 64 == m // 64:
                cm[k, 2, m] = 1.0 / 64.0
    a = np.arange(128)[None, :]
    bb = np.arange(128)[:, None]
    cm[:, 3, :] = (a <= bb).astype(np.float32)
    cm[:, 4, :] = (a >= bb).astype(np.float32)
    S = 384
    mb = np.zeros((128, 4, 5, 128), np.float32)
    for gi, w in enumerate(WINS):
        Mm = np.zeros((S, S), np.float64)
        for t in range(S):
            lo = max(t - w // 2, 0)
            hi = min(t + w - w // 2, S)
            Mm[lo:hi, t] = 1.0 / (hi - lo)
            Mm[t, t] -= 1.0
        mb[:, gi, 0, :] = Mm[0:128, 0:128]
        mb[:, gi, 1, :] = Mm[128:256, 128:256]
        mb[:, gi, 2, :] = Mm[256:384, 256:384]
        mb[:, gi, 3, :] = Mm[0:128, 128:256]
        mb[:, gi, 4, :] = Mm[256:384, 128:256]
    return cm, mb


def _slots(W, ncol_slots):
    return np.ascontiguousarray(W.reshape(8, 128, ncol_slots, 256).transpose(2, 1, 0, 3))


def prepare_shared(inp):
    f = lambda a: np.asarray(a, dtype=np.float32)
    w_mod, b_mod = f(inp["w_mod"]), f(inp["b_mod"])
    sh = {}
    sh["wmod"] = np.ascontiguousarray(w_mod.reshape(DEPTH, 8, 128, 12, 512).transpose(0, 3, 2, 1, 4))
    sh["bmodT"] = np.ascontiguousarray(b_mod.reshape(DEPTH, 48, 128).transpose(2, 0, 1))
    gpre = np.stack([f(inp["g_pre_mix"]), f(inp["g_pre_ffn"])], axis=1)
    sh["gpreT"] = np.ascontiguousarray(gpre.reshape(DEPTH, 2, 8, 128).transpose(3, 0, 1, 2))
    sh["_gpost"] = np.stack([f(inp["g_post_mix"]), f(inp["g_post_ffn"])], axis=1)
    sh["_bgate"] = np.stack([b_mod[:, 2048:3072], b_mod[:, 5120:6144]], axis=1)
    we_in, we_out = f(inp["we_in"]), f(inp["we_out"])
    cols = []
    for c in range(4):
        cols += list(range(c * 64, (c + 1) * 64)) + list(range((c + 4) * 64, (c + 5) * 64))
    qperm_e = np.array(cols)
    colperm = np.concatenate([qperm_e, np.arange(512, 1280)])
    sh["wein"] = np.stack([_slots(we_in[i][:, colperm], 5) for i in range(2)])
    rowperm = np.concatenate([qperm_e, np.arange(512, 1024)])
    sh["weout"] = np.stack([_slots(we_out[i][rowperm, :], 4) for i in range(2)])
    sh["wepool"] = np.ascontiguousarray(f(inp["we_pool"]).transpose(2, 0, 1, 3))
    qg, kg = f(inp["we_q_gain"]), f(inp["we_k_gain"])
    sh["wegain"] = np.ascontiguousarray(np.stack([np.tile(qg, (1, 2)), np.tile(kg, (1, 2))], axis=2).transpose(1, 0, 2))
    sh["wepsc"] = np.ascontiguousarray(f(inp["we_pool_scale"]).reshape(2, 4, 128).transpose(2, 0, 1))
    wo_in, wo_out = f(inp["wo_in"]), f(inp["wo_out"])
    cols = []
    for c in range(8):
        k, r = c // 4, c % 4
        ha, hb = 8 * k + r, 8 * k + r + 4
        cols += list(range(ha * 64, (ha + 1) * 64)) + list(range(hb * 64, (hb + 1) * 64))
    qperm_o = np.array(cols)
    colperm = np.concatenate([qperm_o, np.arange(1024, 1536)])
    sh["woin"] = np.stack([_slots(wo_in[i][:, colperm], 6) for i in range(2)])
    sh["woout"] = np.stack([_slots(wo_out[i][qperm_o, :], 4) for i in range(2)])
    sh["wosink"] = np.ascontiguousarray(np.broadcast_to(f(inp["wo_sink"])[None], (128, 2, 16)))
    wfi, wfo = f(inp["w_ffn_in"]), f(inp["w_ffn_out"])
    g = wfi[:, :, :HID].reshape(DEPTH, 8, 128, NJ, 128)
    u = wfi[:, :, HID:].reshape(DEPTH, 8, 128, NJ, 128)
    sh["wfin"] = np.ascontiguousarray(np.concatenate([g, u], axis=4).transpose(0, 3, 2, 1, 4))
    sh["wfout"] = np.ascontiguousarray(wfo.reshape(DEPTH, NJ, 128, D).transpose(0, 2, 1, 3))
    cosT, sinT = _rope_tables()
    sh["cosT"], sh["sinT"] = cosT, sinT
    sh["cmat"], sh["mband"] = _const_mats()
    return sh


def core_inputs(sh, inp, b0, nb):
    f = lambda a: np.asarray(a, dtype=np.float32)
    NB1 = nb + 1
    m = {k: v for k, v in sh.items() if not k.startswith("_")}
    m["x"] = np.ascontiguousarray(f(inp["x"])[b0:b0 + nb])
    m["ctx"] = np.ascontiguousarray(f(inp["ctx"])[b0:b0 + nb])
    call = np.concatenate([f(inp["c"])[b0:b0 + nb], f(inp["c_ctx"])[None, :]], axis=0)
    m["scT"] = np.ascontiguousarray(call.reshape(NB1, 8, 128).transpose(2, 1, 0))
    m["bgate"] = np.ascontiguousarray(np.broadcast_to(sh["_bgate"][None], (NB1, DEPTH, 2, D)))
    m["gpost"] = np.ascontiguousarray(np.broadcast_to(sh["_gpost"][None], (NB1, DEPTH, 2, D)))
    return m


_CACHE = {}


def kernel(**inputs):
    nb = 32 // N_CORES
    key = ("full", nb)
    if key not in _CACHE:
        _CACHE[key] = build_program(nb, list(range(DEPTH)))[0]
    nc = _CACHE[key]
    sh = prepare_shared(inputs)
    in_maps = [core_inputs(sh, inputs, c * nb, nb) for c in range(N_CORES)]
    res = run_bass_kernel_spmd(nc, in_maps, core_ids=list(range(N_CORES)))
    return np.concatenate([np.asarray(r["out"], dtype=np.float32) for r in res.results], axis=0)
```

```python
import math
import os
DBG = set(os.environ.get('KDBG', '').split(','))
from contextlib import ExitStack
from collections import deque

import numpy as np
import concourse.bass as bass
import concourse.mybir as mybir
from concourse.bass_utils import run_bass_kernel_spmd

F32 = mybir.dt.float32
BF16 = mybir.dt.bfloat16
AF = mybir.ActivationFunctionType
ALU = mybir.AluOpType

D = 1024
SEQ = 2048
CTX = 256
DEPTH = 4
NT = 18
HID = 2816
NJ = 22
EPS = 1e-6
N_CORES = 8
SAME_SYNC = True
WINS = (2, 4, 8, 16)


class SemK:
    __slots__ = ("h", "total")

    def __init__(self, h):
        self.h = h
        self.total = 0


class Tr:
    __slots__ = ("w", "r")

    def __init__(self, base=None):
        self.w = dict(base) if base else {}
        self.r = {}


class Prog:
    ENG = ("pe", "act", "dve", "pool", "sp")

    def __init__(self, nc, es):
        self.nc = nc
        self.es = es
        self.streams = {e: [] for e in self.ENG}
        self.cnt = {e: 0 for e in self.ENG}
        self.waited = {e: {} for e in self.ENG}
        self.esem = {}
        for e in ("pe", "act", "dve", "pool"):
            self.esem[e] = self.new_sem("e_" + e)
        self.trs = {}
        self.ubase = {}
        self.ukeys = set()
        self.n_inst = 0

    def new_sem(self, name):
        return SemK(self.es.enter_context(self.nc.semaphore(name)))

    def T(self, *key):
        t = self.trs.get(key)
        if t is None:
            t = Tr(self.ubase if key[0] in self.ukeys else None)
            self.trs[key] = t
        return t

    def retire(self, names):
        for key in [k for k in self.trs if k[0] in names]:
            t = self.trs.pop(key)
            for d in (t.w, t.r):
                for s, v in d.items():
                    if self.ubase.get(s, 0) < v:
                        self.ubase[s] = v

    def emit(self, eng, fn, reads=(), writes=(), signal=True, dsem=None):
        deps = {}
        for t in reads:
            for s, v in t.w.items():
                if deps.get(s, 0) < v:
                    deps[s] = v
        for t in writes:
            for d in (t.w, t.r):
                for s, v in d.items():
                    if deps.get(s, 0) < v:
                        deps[s] = v
        own = self.esem.get(eng)
        wd = self.waited[eng]
        waits = []
        for s, v in deps.items():
            if s is own and (eng == "pe" or not SAME_SYNC):
                continue
            if wd.get(s, 0) >= v:
                continue
            wd[s] = v
            waits.append((s.h, v))
        if dsem is not None:
            dsem.total += 16
            tok_s, tok_v = dsem, dsem.total
            sig = False
        else:
            tok_s, tok_v = own, self.cnt[eng] + 1
            sig = signal
            if signal:
                self.cnt[eng] += 1
        for t in reads:
            if t.r.get(tok_s, 0) < tok_v:
                t.r[tok_s] = tok_v
        for t in writes:
            t.w = {tok_s: tok_v}
            t.r = {}
        self.streams[eng].append((waits, fn, sig))
        self.n_inst += 1 + len(waits)

    def dma(self, eng, out, in_, dsem, reads=(), writes=()):
        h = dsem.h
        self.emit(eng, lambda e: e.dma_start(out=out, in_=in_).then_inc(h, 16),
                  reads=reads, writes=writes, dsem=dsem)


    def dma_group(self, items, dsem, reads=(), writes=()):
        deps = {}
        for t in reads:
            for s, v in t.w.items():
                if deps.get(s, 0) < v:
                    deps[s] = v
        for t in writes:
            for d in (t.w, t.r):
                for s, v in d.items():
                    if deps.get(s, 0) < v:
                        deps[s] = v
        h = dsem.h
        for (eng, out, in_) in items:
            wd = self.waited[eng]
            waits = []
            for s, v in deps.items():
                if wd.get(s, 0) >= v:
                    continue
                wd[s] = v
                waits.append((s.h, v))
            dsem.total += 16
            self.streams[eng].append((waits, (lambda out=out, in_=in_: lambda e: e.dma_start(out=out, in_=in_).then_inc(h, 16))(), False))
            self.n_inst += 1 + len(waits)
        for t in reads:
            if t.r.get(dsem, 0) < dsem.total:
                t.r[dsem] = dsem.total
        for t in writes:
            t.w = {dsem: dsem.total}
            t.r = {}

    def play(self, final_waits):
        nc = self.nc
        with nc.Block() as block:
            def mk(ename):
                stream = self.streams[ename]
                own = self.esem.get(ename)

                def body(e):
                    for waits, fn, sig in stream:
                        for h, v in waits:
                            e.wait_ge(h, v)
                        ins = fn(e)
                        if sig:
                            ins.then_inc(own.h, 1)
                    if ename == "sp":
                        for s in final_waits:
                            if s.total:
                                e.wait_ge(s.h, s.total)
                return body
            block.tensor(mk("pe"))
            block.scalar(mk("act"))
            block.vector(mk("dve"))
            block.gpsimd(mk("pool"))
            block.sync(mk("sp"))


def MM(out, lhsT, rhs, st=True, sp=True):
    return lambda e: e.matmul(out, lhsT, rhs, start=st, stop=sp)


def TP(out, in_, ident):
    return lambda e: e.transpose(out, in_, ident)


def ACT(out, in_, func, **kw):
    return lambda e: e.activation(out=out, in_=in_, func=func, **kw)


def ACP(out, in_):
    return lambda e: e.copy(out=out, in_=in_)


def TT(out, a, b, op):
    return lambda e: e.tensor_tensor(out=out, in0=a, in1=b, op=op)


def TS(out, in0, s1, s2=None, op0=ALU.mult, op1=None):
    if op1 is None:
        return lambda e: e.tensor_scalar(out=out, in0=in0, scalar1=s1, scalar2=None, op0=op0)
    return lambda e: e.tensor_scalar(out=out, in0=in0, scalar1=s1, scalar2=s2, op0=op0, op1=op1)


def STT(out, in0, scalar, in1, op0, op1):
    return lambda e: e.scalar_tensor_tensor(out=out, in0=in0, scalar=scalar, in1=in1, op0=op0, op1=op1)


def RCP(out, in_):
    return lambda e: e.reciprocal(out=out, in_=in_)


def VCP(out, in_):
    return lambda e: e.tensor_copy(out=out, in_=in_)


def MSET(ap, v):
    return lambda e: e.memset(ap, v)


def layer_info(l):
    even = (l % 2 == 0)
    return dict(even=even, i=l // 2, nq=4 if even else 8, nk=1 if even else 2,
                nslot_in=5 if even else 6, with_ctx=(l < DEPTH - 1))


def weight_schedule(nb, layers):
    out = []
    for b in range(nb):
        for l in layers:
            li = layer_info(l)
            for n in range(5):
                for s in range(li["nslot_in"]):
                    want_q = (n < 4) or li["with_ctx"]
                    if not want_q and s < li["nq"] // 2:
                        continue
                    if not want_q and li["even"] and s >= 3:
                        continue
                    out.append(("in", l, n, s))
            for s in range(4):
                out.append(("out", l, 0, s))
            nblk = 5 if li["with_ctx"] else 4
            for n in range(nblk):
                for j in range(NJ):
                    out.append(("ffn", l, n, j))
    return out


def build_program(nb, layers):
    NB1 = nb + 1
    nc = bass.Bass("TRN2", target_bir_lowering=False)
    es = ExitStack()

    def din(name, shape, dt=F32):
        return nc.dram_tensor(name, list(shape), dt, kind="ExternalInput").ap()

    x_d = din("x", [nb, SEQ, D])
    ctx_d = din("ctx", [nb, CTX, D])
    scT_d = din("scT", [128, 8, NB1])
    wmod_d = din("wmod", [DEPTH, 12, 128, 8, 512])
    bmodT_d = din("bmodT", [128, DEPTH, 48])
    bgate_d = din("bgate", [NB1, DEPTH, 2, D])
    gpost_d = din("gpost", [NB1, DEPTH, 2, D])
    gpreT_d = din("gpreT", [128, DEPTH, 2, 8])
    wein_d = din("wein", [2, 5, 128, 8, 256])
    weout_d = din("weout", [2, 4, 128, 8, 256])
    wepool_d = din("wepool", [128, 2, 4, 128])
    wegain_d = din("wegain", [128, 2, 2])
    wepsc_d = din("wepsc", [128, 2, 4])
    woin_d = din("woin", [2, 6, 128, 8, 256])
    woout_d = din("woout", [2, 4, 128, 8, 256])
    wosink_d = din("wosink", [128, 2, 16])
    wfin_d = din("wfin", [DEPTH, NJ, 128, 8, 256])
    wfout_d = din("wfout", [DEPTH, 128, NJ, D])
    cos_d = din("cosT", [128, SEQ])
    sin_d = din("sinT", [128, SEQ])
    cmat_d = din("cmat", [128, 5, 128])
    mband_d = din("mband", [128, 4, 5, 128])
    out_d = nc.dram_tensor("out", [nb, SEQ, D], F32, kind="ExternalOutput").ap()
    gsc_d = nc.dram_tensor("gsc", [DEPTH, 2, NB1, D], F32, kind="Internal").ap()

    P = Prog(nc, es)
    T = P.T

    def sb(name, shape, dt):
        return es.enter_context(nc.sbuf_tensor(name, list(shape), dt))

    def ps(name, shape, dt):
        return es.enter_context(nc.psum_tensor(name, list(shape), dt))

    xs = sb("xs", [128, NT, D], F32)
    U = sb("U", [128, 16896], F32)
    hT = sb("hT", [128, 8, 512], BF16)
    NSLOT = 4
    wsl = sb("wsl", [128, NSLOT, 8, 256], BF16)
    cos_sb = sb("cos_sb", [128, SEQ], BF16)
    sin_sb = sb("sin_sb", [128, SEQ], BF16)
    GG = sb("GG", [128, 2, D], F32)
    xn2 = sb("xn2", [128, 2, D], BF16)
    tmp = sb("tmp", [128, D], F32)
    tmpb = tmp[:, :].bitcast(BF16)
    PT = sb("PT", [128, 3, 512], BF16)
    r_sd = sb("r_sd", [128, 512], F32)
    r_qn = sb("r_qn", [128, 512], BF16)
    r_t1 = sb("r_t1", [128, 512], F32)
    r_t2 = sb("r_t2", [128, 512], F32)
    r_sq = r_qn
    r_t2b = r_t2[:, :].bitcast(BF16)
    osb = r_t1
    rcp = r_sd
    cmat = sb("cmat_sb", [128, 5, 128], BF16)
    mband = sb("mband_sb", [128, 4, 5, 128], BF16)
    wpool = sb("wpool_sb", [128, 4, 128], BF16)
    onesel = sb("onesel", [128, 2, 128], BF16)
    stat = sb("stat", [128, 3, 4], F32)
    scT = sb("scT_sb", [128, 8, NB1], F32)
    AT = sb("AT", [128, DEPTH, 2, 8, NB1], F32)
    BT = sb("BT", [128, DEPTH, 2, 8, NB1], F32)
    wegain = sb("wegain_sb", [128, 2, 2], F32)
    wepsc = sb("wepsc_sb", [128, 2, 4], F32)
    esink = sb("esink", [128, 2, 16], F32)

    Ub = U[:, :].bitcast(BF16)
    QA = Ub[:, 0:18432].rearrange("p (c t) -> p c t", c=8)
    KT = Ub[:, 18432:23040].rearrange("p (c t) -> p c t", c=2)
    VA = Ub[:, 23040:29952].rearrange("p (t v) -> p t v", t=NT)
    KTe = Ub[:, 18432:20736].rearrange("p (c t) -> p c t", c=1)
    VAe = Ub[:, 20736:24192].rearrange("p (t v) -> p t v", t=NT)
    UT = Ub[:, 24192:33408].rearrange("p (t v) -> p t v", t=NT)
    w2 = Ub[:, 0:22528].rearrange("p (j n) -> p j n", j=NJ)
    mT = Ub[:, 22528:33792].rearrange("p (j n) -> p j n", j=NJ)
    wmst = U[:, 0:8192].rearrange("p (s k n) -> p s k n", s=2, k=8)
    grow = U[0:NB1, 8192:9728].rearrange("p (a n) -> p a n", a=3)
    modT = U[:, 9728:9728 + DEPTH * 48 * NB1].rearrange("p (l c b) -> p l c b", l=DEPTH, c=48)
    o_ = 9728 + DEPTH * 48 * NB1
    bmodT = U[:, o_:o_ + DEPTH * 48].rearrange("p (l c) -> p l c", l=DEPTH)
    gpreT = U[:, o_ + DEPTH * 48:o_ + DEPTH * 48 + DEPTH * 16].rearrange("p (l w c) -> p l w c", l=DEPTH, w=2)
    o2_ = o_ + DEPTH * 48 + DEPTH * 16
    scTb = U[:, o2_:o2_ + 4 * NB1].bitcast(BF16).rearrange("p (k b) -> p k b", k=8)
    wmst16 = U[:, 0:8192].bitcast(BF16).rearrange("p (s k n) -> p s k n", s=4, k=8)
    P.ukeys = {"QA", "KT", "VA", "UT", "w2", "mT", "wmst", "grow", "modT", "stc", "scTb"}
    MIX_KEYS = {"QA", "KT", "VA", "UT"}
    FFN_KEYS = {"w2", "mT"}

    psT = ps("psT", [128, 8, 128], BF16)
    psA = ps("psA", [128, 512], F32)
    psB = ps("psB", [128, 1024], F32)
    psC = ps("psC", [128, 1024], F32)
    psD = ps("psD", [128, 1024], F32)
    bankB = [(psB[:, 0:512], ("psB", 0)), (psB[:, 512:1024], ("psB", 1))]
    bankC = [(psC[:, 0:512], ("psC", 0)), (psC[:, 512:1024], ("psC", 1))]
    bankS = bankB + [(psD[:, 0:512], ("psD", 0)), (psD[:, 512:1024], ("psD", 1))]

    s_const = P.new_sem("s_const")
    s_constb = P.new_sem("s_constb")
    s_x = [P.new_sem("s_x%d" % t) for t in range(NT)]
    s_out = [P.new_sem("s_o%d" % t) for t in range(16)]
    s_slot = [P.new_sem("s_ws%d" % s) for s in range(NSLOT)]
    s_w2 = P.new_sem("s_w2")
    s_wm = [P.new_sem("s_wm%d" % k_) for k_ in range(4)]
    s_gg = [P.new_sem("s_gg0"), P.new_sem("s_gg1")]
    s_gs = P.new_sem("s_gs")
    s_wp = P.new_sem("s_wp")
    s_stc = P.new_sem("s_stc")
    s_gr = [P.new_sem("s_gr1"), P.new_sem("s_gr2")]

    ident = cmat[:, 0, :]
    perm = cmat[:, 1, :]
    blkm = cmat[:, 2, :]
    maskL = cmat[:, 3, :]
    maskR = cmat[:, 4, :]

    cT = T("const")
    P.dma_group([
        ("pool", cmat[:], cmat_d[:, :, :]),
        ("pool", mband[:], mband_d[:, :, :, :]),
        ("pool", cos_sb[:], cos_d[:, :]),
        ("pool", sin_sb[:], sin_d[:, :]),
    ], s_const, writes=[cT])
    cTb = T("constb")
    P.dma_group([
        ("sp", scT[:], scT_d[:, :, :]),
        ("sp", wegain[:], wegain_d[:, :, :]),
        ("sp", wepsc[:], wepsc_d[:, :, :]),
        ("sp", esink[:], wosink_d[:, :, :]),
    ], s_constb, writes=[cTb])
    P.emit("act", ACT(scT[:], scT[:], AF.Silu), reads=[cTb], writes=[cTb])
    P.emit("act", ACT(esink[:], esink[:], AF.Exp), reads=[cTb], writes=[cTb])
    P.emit("act", ACP(stat[:, 0, 0:1], esink[:, 0, 0:1]), reads=[cT, cTb], writes=[cT])
    P.dma_group([("sp", bmodT, bmodT_d[:, :, :]), ("sp", gpreT, gpreT_d[:, :, :, :])], s_stc, writes=[T("stc")])
    c2 = T("const2")
    P.emit("dve", MSET(onesel[:], 0.0), writes=[c2])
    P.emit("dve", MSET(onesel[:, 0, 0:64], 1.0), writes=[c2])
    P.emit("dve", MSET(onesel[:, 1, 64:128], 1.0), writes=[c2])

    GATE_CB = {4: (0, 0), 5: (0, 1), 10: (1, 0), 11: (1, 1)}
    P.emit("dve", VCP(scTb, scT[:]), reads=[cT], writes=[T("scTb")])
    cnt_wm = 0
    for l in layers:
        for cb in range(12):
            sl = cnt_wm % 4
            cnt_wm += 1
            P.dma("pool", wmst16[:, sl], wmod_d[l, cb], s_wm[sl], writes=[T("wmst", sl)])
            if cb in GATE_CB:
                which, half = GATE_CB[cb]
                hs = slice(half * 512, (half + 1) * 512)
                for kc in range(8):
                    P.emit("pe", MM(psB[0:NB1, 0:512], scTb[:, kc, :], wmst16[:, sl, kc, :], kc == 0, kc == 7),
                           reads=[T("scTb"), T("wmst", sl)], writes=[T("psB", 0)], signal=(kc == 7))
                P.dma("sp", grow[:, 1, :], bgate_d[:, l, which, hs], s_gr[0], writes=[T("grow", 1)])
                P.dma("sp", grow[:, 2, :], gpost_d[:, l, which, hs], s_gr[1], writes=[T("grow", 2)])
                P.emit("dve", TT(grow[:, 0, :], psB[0:NB1, 0:512], grow[:, 1, :], ALU.add),
                       reads=[T("psB", 0), T("grow", 1)], writes=[T("grow", 0)])
                P.emit("dve", TT(grow[:, 0, :], grow[:, 0, :], grow[:, 2, :], ALU.mult),
                       reads=[T("grow", 2)], writes=[T("grow", 0)])
                P.dma("sp", gsc_d[l, which, :, hs], grow[:, 0, :], s_gs, reads=[T("grow", 0)], writes=[T("gsc")])
            else:
                for fc in range(4):
                    ch = cb * 4 + fc
                    pa = psA[:, fc * NB1:(fc + 1) * NB1]
                    for kc in range(8):
                        P.emit("pe", MM(pa, wmst16[:, sl, kc, fc * 128:(fc + 1) * 128], scTb[:, kc, :], kc == 0, kc == 7),
                               reads=[T("scTb"), T("wmst", sl)], writes=[T("psA")], signal=(kc == 7))
                    P.emit("dve", TS(modT[:, l, ch, :], pa, bmodT[:, l, ch:ch + 1], op0=ALU.add),
                           reads=[T("psA"), T("stc")], writes=[T("modT")])
        for which, sc0 in ((0, 8), (1, 32)):
            for c in range(8):
                P.emit("dve", TS(AT[:, l, which, c, :], modT[:, l, sc0 + c, :], 1.0, gpreT[:, l, which, c:c + 1],
                                 op0=ALU.add, op1=ALU.mult),
                       reads=[T("modT"), T("stc")], writes=[T("AT")])
            sh0_ = 0 if which == 0 else 24
            P.emit("dve", VCP(BT[:, l, which, :, :], modT[:, l, sh0_:sh0_ + 8, :]), reads=[T("modT")], writes=[T("AT")])
    P.retire({"wmst", "grow", "modT", "stc", "scTb"})

    sched = deque(weight_schedule(nb, layers))
    inflight = deque()
    ring = {"issued": 0, "released": 0, "popped": 0}

    def w_src(key):
        kind, l, n, s = key
        li = layer_info(l)
        if kind == "in":
            return (wein_d if li["even"] else woin_d)[li["i"], s]
        if kind == "out":
            return (weout_d if li["even"] else woout_d)[li["i"], s]
        return wfin_d[l, s]

    def w_prefetch():
        while sched and ring["issued"] - ring["released"] < NSLOT:
            key = sched.popleft()
            s = ring["issued"] % NSLOT
            ring["issued"] += 1
            P.dma("pool", wsl[:, s], w_src(key), s_slot[s], writes=[T("wsl", s)])
            inflight.append((key, s))

    def w_pop(kind, l, n, s):
        w_prefetch()
        key, slot = inflight.popleft()
        assert key == (kind, l, n, s), (key, (kind, l, n, s))
        ring["popped"] += 1
        return slot

    def w_release(k=1):
        ring["released"] += k
        assert ring["released"] <= ring["popped"]
        w_prefetch()

    ev = {"i": 0}

    def blocks_of():
        bl = [(n, n * 512, 512, list(range(4 * n, 4 * n + 4))) for n in range(4)]
        bl.append((4, SEQ, 256, [16, 17]))
        return bl

    hT2 = GG[:, :, :].rearrange("p a n -> p (a n)").bitcast(BF16).rearrange("p (c t) -> p c t", c=8)
    HB = [(hT, "hT"), (hT2, "hT2")]
    hcur = {"b": HB[0]}
    xn_state = {"i": 0}

    def norm_pre(b, l, which, t):
        col = ev["i"] % 4
        ev["i"] += 1
        xi = xn_state["i"] % 2
        xn_state["i"] += 1
        xb = xn2[:, xi, :]
        xt = T("x", t)
        sT_ = T("stat", col)
        P.emit("act", ACT(xb, xs[:, t, :], AF.Square, accum_out=stat[:, 0, col:col + 1]),
               reads=[xt], writes=[sT_, T("xn", xi)])
        P.emit("act", ACT(stat[:, 1, col:col + 1], stat[:, 0, col:col + 1], AF.Sqrt, scale=1.0 / D, bias=eps_ap),
               reads=[c2], writes=[sT_])
        P.emit("dve", RCP(stat[:, 2, col:col + 1], stat[:, 1, col:col + 1]), writes=[sT_])
        P.emit("dve", TS(xb, xs[:, t, :], stat[:, 2, col:col + 1]), reads=[xt, sT_], writes=[T("xn", xi)])
        return xi

    def norm_post(b, l, which, t, k, xi, hb=None):
        hb = hb or HB[0]
        bsel = nb if t >= 16 else b
        for c in range(8):
            P.emit("pe", TP(psT[:, c, :], xn2[:, xi, c * 128:(c + 1) * 128], ident),
                   reads=[T("xn", xi), cT], writes=[T("psT")], signal=(c == 7))
        for c in range(8):
            a_ap = AT[:, l, which, c, bsel:bsel + 1]
            b_ap = BT[:, l, which, c, bsel:bsel + 1]
            dst = hb[0][:, c, k * 128:(k + 1) * 128]
            wr = [T(hb[1], k)]
            if hb[1] == "hT2" and k == 0 and c == 0:
                wr = wr + [T("GG", 0), T("GG", 1)]
            if c % 2 == 0:
                P.emit("act", ACT(dst, psT[:, c, :], AF.Identity, scale=a_ap, bias=b_ap),
                       reads=[T("psT"), T("AT")], writes=wr)
            else:
                P.emit("dve", TS(dst, psT[:, c, :], a_ap, b_ap, op0=ALU.mult, op1=ALU.add),
                       reads=[T("psT"), T("AT")], writes=wr)

    def norm_block(b, l, which, tiles, hb=None):
        xi = norm_pre(b, l, which, tiles[0])
        for k, t in enumerate(tiles):
            nxt = norm_pre(b, l, which, tiles[k + 1]) if k + 1 < len(tiles) else None
            norm_post(b, l, which, t, k, xi, hb)
            xi = nxt

    def norm_thunks(b, l, which, tiles, hb):
        st_ = {}
        th = []

        def first():
            st_["xi"] = norm_pre(b, l, which, tiles[0])
        th.append(first)
        for k, t in enumerate(tiles):
            def step(k=k, t=t):
                nxt = norm_pre(b, l, which, tiles[k + 1]) if k + 1 < len(tiles) else None
                norm_post(b, l, which, t, k, st_["xi"], hb)
                st_["xi"] = nxt
            th.append(step)
        return deque(th)

    pp_state = {"i": 0}

    def proj_fm(slot, off, N, ntl):
        ap_, key = bankB[pp_state["i"] % 2]
        pp_state["i"] += 1
        for kc in range(8):
            P.emit("pe", MM(ap_[:, 0:N], wsl[:, slot, kc, off:off + 128], hcur["b"][0][:, kc, 0:N], kc == 0, kc == 7),
                   reads=[T("wsl", slot)] + [T(hcur["b"][1], k) for k in range(ntl)], writes=[T(*key)], signal=(kc == 7))
        if hcur.get("extras"):
            hcur["extras"].popleft()()
        return ap_, key

    def qk_post(li, pp, pkey, N, s0, is_ctx, gain_ap, dst, dst_tr):
        ppt = T(*pkey)
        if 'noqk' in DBG:
            return
        cs = cos_sb[:, s0:s0 + N] if not is_ctx else None
        sn = sin_sb[:, s0:s0 + N] if not is_ctx else None
        if li["even"]:
            sqb = PT[:, 0, :]
            P.emit("act", ACT(sqb[:, 0:N], pp[:, 0:N], AF.Square), reads=[ppt], writes=[T("PT", 0)])
            P.emit("dve", TS(r_qn[:, 0:N], pp[:, 0:N], gain_ap), reads=[ppt, cT, T("PT", 0)], writes=[T("r_qn")])
            P.emit("pe", MM(psC[:, 0:N], blkm, sqb[:, 0:N]), reads=[T("PT", 0), cT], writes=[T("psC", 0)])
            if not is_ctx:
                P.emit("pe", MM(psC[:, 512:512 + N], perm, r_qn[:, 0:N]), reads=[T("r_qn"), cT], writes=[T("psC", 1)])
            P.emit("act", ACT(r_sd[:, 0:N], psC[:, 0:N], AF.Sqrt, bias=eps_ap, scale=1.0),
                   reads=[T("psC", 0), c2], writes=[T("r_sd")])
            P.emit("dve", RCP(r_sd[:, 0:N], r_sd[:, 0:N]), writes=[T("r_sd")])
            if is_ctx:
                P.emit("dve", TT(dst, r_qn[:, 0:N], r_sd[:, 0:N], ALU.mult), reads=[T("r_qn"), T("r_sd")], writes=[dst_tr])
                return
            P.emit("dve", TT(r_t1[:, 0:N], r_qn[:, 0:N], cs, ALU.mult), reads=[T("r_qn"), cT], writes=[T("r_t1")])
            P.emit("dve", TT(r_t2[:, 0:N], psC[:, 512:512 + N], sn, ALU.mult), reads=[T("psC", 1), cT], writes=[T("r_t2")])
            P.emit("dve", TT(r_t1[:, 0:N], r_t1[:, 0:N], r_t2[:, 0:N], ALU.add), reads=[T("r_t2")], writes=[T("r_t1")])
            P.emit("dve", TT(dst, r_t1[:, 0:N], r_sd[:, 0:N], ALU.mult), reads=[T("r_t1"), T("r_sd")], writes=[dst_tr])
            return
        else:
            if is_ctx:
                P.emit("act", ACP(dst, pp[:, 0:N]), reads=[ppt], writes=[dst_tr])
                return
            P.emit("act", ACP(r_qn[:, 0:N], pp[:, 0:N]), reads=[ppt], writes=[T("r_qn")])
            P.emit("pe", MM(psC[:, 512:512 + N], perm, r_qn[:, 0:N]), reads=[T("r_qn"), cT], writes=[T("psC", 1)])
            P.emit("dve", TT(r_t1[:, 0:N], r_qn[:, 0:N], cs, ALU.mult), reads=[T("r_qn"), cT], writes=[T("r_t1")])
        P.emit("dve", TT(r_t2[:, 0:N], psC[:, 512:512 + N], sn, ALU.mult), reads=[T("psC", 1), cT], writes=[T("r_t2")])
        P.emit("dve", TT(dst, r_t1[:, 0:N], r_t2[:, 0:N], ALU.add), reads=[T("r_t1"), T("r_t2")], writes=[dst_tr])

    def in_proj_block(b, l, li, blk):
        n, s0, N, tiles = blk
        is_ctx = (n == 4)
        ntl = len(tiles)
        want_q = (not is_ctx) or li["with_ctx"]
        even = li["even"]
        i = li["i"]
        kt = KTe if even else KT
        va = VAe if even else VA
        nqs = li["nq"] // 2
        st = {"vslot": None, "voff": 0, "VW": 0}

        def chunk_iter():
            if want_q:
                for s in range(nqs):
                    slot = w_pop("in", l, n, s)
                    for hf in range(2):
                        c = 2 * s + hf
                        yield (slot, hf * 128, hf == 1,
                               (wegain[:, i, 0:1] if even else None, QA[:, c, s0:s0 + N], T("QA", c, n)))
            if even:
                st["vslot"] = w_pop("in", l, n, 2)
                st["voff"], st["VW"] = 128, 128
                yield (st["vslot"], 0, False, (wegain[:, i, 1:2], kt[:, 0, s0:s0 + N], T("KT", 0, n)))
            else:
                slot = w_pop("in", l, n, 4)
                for hf in range(2):
                    yield (slot, hf * 128, hf == 1, (None, kt[:, hf, s0:s0 + N], T("KT", hf, n)))

        prev = None
        for (slot, off, rel, pa) in chunk_iter():
            pp, pkey = proj_fm(slot, off, N, ntl)
            if rel:
                w_release()
            if prev is not None:
                qk_post(*prev)
            prev = (li, pp, pkey, N, s0, is_ctx) + pa
        if not even:
            st["vslot"] = w_pop("in", l, n, 5)
            st["voff"], st["VW"] = 0, 256
        vslot, voff, VW = st["vslot"], st["voff"], st["VW"]
        for k, t in enumerate(tiles):
            if k == 1 and prev is not None:
                qk_post(*prev)
                prev = None
            for kc in range(8):
                P.emit("pe", MM(psA[:, 0:VW], hcur["b"][0][:, kc, k * 128:(k + 1) * 128], wsl[:, vslot, kc, voff:voff + VW],
                                kc == 0, kc == 7),
                       reads=[T("wsl", vslot), T(hcur["b"][1], k)], writes=[T("psA")], signal=(kc == 7))
            if even:
                P.emit("act", ACP(va[:, t, 0:64], psA[:, 0:64]), reads=[T("psA")], writes=[T("VA", t)])
                P.emit("dve", VCP(va[:, t, 128:192], psA[:, 64:128]), reads=[T("psA")], writes=[T("VA", t)])
            else:
                for gi, vb_ in enumerate((0, 128, 192, 320)):
                    if gi % 2 == 0:
                        P.emit("act", ACP(va[:, t, vb_:vb_ + 64], psA[:, gi * 64:(gi + 1) * 64]), reads=[T("psA")], writes=[T("VA", t)])
                    else:
                        P.emit("dve", VCP(va[:, t, vb_:vb_ + 64], psA[:, gi * 64:(gi + 1) * 64]), reads=[T("psA")], writes=[T("VA", t)])
        if prev is not None:
            qk_post(*prev)
            prev = None
        w_release()
        if even and want_q:
            s3 = w_pop("in", l, n, 3)
            s4 = w_pop("in", l, n, 4)
            for k, t in enumerate(tiles):
                ap_, key = bankB[pp_state["i"] % 2]
                pp_state["i"] += 1
                for hs, slot in enumerate((s3, s4)):
                    for kc in range(8):
                        P.emit("pe", MM(ap_[:, hs * 256:(hs + 1) * 256], hcur["b"][0][:, kc, k * 128:(k + 1) * 128],
                                        wsl[:, slot, kc, :], kc == 0, kc == 7),
                               reads=[T("wsl", slot), T(hcur["b"][1], k)], writes=[T(*key)], signal=(kc == 7))
                if k % 2 == 0:
                    P.emit("act", ACP(UT[:, t, :], ap_[:, 0:512]), reads=[T(*key)], writes=[T("UT", t)])
                else:
                    P.emit("dve", VCP(UT[:, t, :], ap_[:, 0:512]), reads=[T(*key)], writes=[T("UT", t)])
            w_release(2)

    def pool_mixer(b, l, li):
        i = li["i"]
        P.dma("pool", wpool[:], wepool_d[:, i, :, :], s_wp, writes=[T("wpool")])
        for g in range(4):
            for blk in blocks_of():
                n, s0, N, tiles = blk
                if n == 4 and not li["with_ctx"]:
                    continue
                first, last = (16, 17) if n == 4 else (0, 15)
                ap_, key = bankB[pp_state["i"] % 2]
                pp_state["i"] += 1
                for k, t in enumerate(tiles):
                    contrib = []
                    if t > first:
                        contrib.append((t - 1, 3))
                    contrib.append((t, 0 if t == first else (2 if t == last else 1)))
                    if t < last:
                        contrib.append((t + 1, 4))
                    for ci, (s, kind) in enumerate(contrib):
                        lastmm = (ci == len(contrib) - 1)
                        P.emit("pe", MM(ap_[:, k * 128:(k + 1) * 128], UT[:, s, g * 128:(g + 1) * 128],
                                        mband[:, g, kind, :], ci == 0, lastmm),
                               reads=[T("UT", s), cT], writes=[T(*key)], signal=(lastmm and k == len(tiles) - 1))
                P.emit("act", ACP(r_qn[:, 0:N], ap_[:, 0:N]), reads=[T(*key)], writes=[T("r_qn")])
                P.emit("pe", MM(psC[:, 0:N], wpool[:, g, :], r_qn[:, 0:N]), reads=[T("r_qn"), T("wpool")], writes=[T("psC", 0)])
                P.emit("dve", TS(QA[:, 4 + g, s0:s0 + N], psC[:, 0:N], wepsc[:, i, g:g + 1]),
                       reads=[T("psC", 0), cT], writes=[T("QA", 4 + g, n)])

    at_state = {"s": 0, "o": 0, "pt": 0}

    def attention(b, l, li):
        even = li["even"]
        i = li["i"]
        kt = KTe if even else KT
        va = VAe if even else VA
        units = []
        for c in range(li["nq"]):
            for p in range(2):
                for blk in blocks_of():
                    if blk[0] == 4 and not li["with_ctx"]:
                        continue
                    units.append((c, p, blk))

        def unit_setup(c, p, blk):
            n, s0, N, tiles = blk
            u = dict(c=c, p=p, n=n, s0=s0, N=N, p0=p * 64)
            if even:
                u["kch"], g, u["head"] = 0, p, c + 4 * p
                u["vbase"] = 0 if g == 0 else 64
            else:
                kk, r = c // 4, c % 4
                g = 2 * kk + p
                u["head"] = 8 * kk + r + 4 * p
                u["kch"] = kk
                u["vbase"] = (0, 64, 192, 256)[g]
            u["M"] = 128
            u["dp"] = 64 if p == 0 else 0
            if n == 4:
                kl = [(16, 0, N, None), (17, 0, N, None)]
            elif even:
                kl = [(j, 0, N, None) for j in range(NT)]
            else:
                kl = [(16, 0, N, None), (17, 0, N, None)]
                t0 = 4 * n
                for j in range(max(t0 - 1, 0), min(t0 + 4, 15) + 1):
                    lo = max(j - 1, t0)
                    hi = min(j + 1, t0 + 3)
                    masks = []
                    if j - 1 >= t0:
                        masks.append((j - 1 - lo, maskR))
                    if j + 1 <= t0 + 3:
                        masks.append((j + 1 - lo, maskL))
                    kl.append((j, (lo - t0) * 128, (hi - t0 + 1) * 128, masks))
            u["kl"] = kl
            u["po"], u["pokey"] = bankC[at_state["o"] % 2]
            at_state["o"] += 1
            u["qtr"] = T("QA", c, n, p)
            return u

        def emit_S(u, idx):
            j, qa, qb, masks = u["kl"][idx]
            c, n, p0, s0, kch = u["c"], u["n"], u["p0"], u["s0"], u["kch"]
            pS, pSkey = bankS[at_state["s"] % 4]
            at_state["s"] += 1
            pti = at_state["pt"] % 3
            at_state["pt"] += 1
            W = qb - qa
            P.emit("pe", MM(pS[:, 0:W], kt[p0:p0 + 64, kch, j * 128:(j + 1) * 128],
                            QA[p0:p0 + 64, c, s0 + qa:s0 + qb]),
                   reads=[T("KT", kch, 4 if j >= 16 else j // 4), T("QA", c, n), u["qtr"]], writes=[T(*pSkey)])
            P.emit("act", ACT(PT[:, pti, 0:W], pS[:, 0:W], AF.Exp, scale=0.125),
                   reads=[T(*pSkey)], writes=[T("PT", pti)])
            for (mt, mk) in (masks or []):
                P.emit("dve", TT(PT[:, pti, mt * 128:(mt + 1) * 128], PT[:, pti, mt * 128:(mt + 1) * 128], mk, ALU.mult),
                       reads=[cT], writes=[T("PT", pti)])
            return pti

        def emit_PV(u, idx, pti):
            j, qa, qb, masks = u["kl"][idx]
            W = qb - qa
            lastk = (idx == len(u["kl"]) - 1)
            M, vbase = u["M"], u["vbase"]
            P.emit("pe", MM(u["po"][0:M, qa:qb], va[:, j, vbase:vbase + M], PT[:, pti, 0:W], idx == 0, lastk),
                   reads=[T("VA", j), T("PT", pti)], writes=[T(*u["pokey"])], signal=lastk)

        def emit_norm(u):
            po, pokey, N, dp, p0, p, c, s0 = u["po"], u["pokey"], u["N"], u["dp"], u["p0"], u["p"], u["c"], u["s0"]
            if even:
                P.emit("dve", RCP(rcp[dp:dp + 1, 0:N], po[dp:dp + 1, 0:N]), reads=[T(*pokey)], writes=[T("r_sd")])
            else:
                P.emit("dve", TS(rcp[dp:dp + 1, 0:N], po[dp:dp + 1, 0:N], esink[dp:dp + 1, i, u["head"]:u["head"] + 1], op0=ALU.add),
                       reads=[T(*pokey), cT], writes=[T("r_sd")])
                P.emit("dve", RCP(rcp[dp:dp + 1, 0:N], rcp[dp:dp + 1, 0:N]), writes=[T("r_sd")])
            rhi = r_qn[dp:dp + 1, 0:N]
            rlo = r_t2b[dp:dp + 1, 0:N]
            P.emit("dve", VCP(rhi, rcp[dp:dp + 1, 0:N]), reads=[T("r_sd")], writes=[T("r_qn")])
            P.emit("dve", TT(rlo, rcp[dp:dp + 1, 0:N], rhi, ALU.subtract), reads=[T("r_sd"), T("r_qn")], writes=[T("r_t2")])
            P.emit("pe", MM(psA[:, 0:N], onesel[dp:dp + 1, p, :], rhi, True, False),
                   reads=[T("r_qn"), c2], writes=[T("psA")], signal=False)
            P.emit("pe", MM(psA[:, 0:N], onesel[dp:dp + 1, p, :], rlo, False, True),
                   reads=[T("r_t2"), c2], writes=[T("psA")])
            P.emit("act", ACP(osb[p0:p0 + 64, 0:N], po[p0:p0 + 64, 0:N]), reads=[T(*pokey)], writes=[T("r_t1")])
            P.emit("dve", TT(QA[p0:p0 + 64, c, s0:s0 + N], osb[p0:p0 + 64, 0:N], psA[p0:p0 + 64, 0:N], ALU.mult),
                   reads=[T("r_t1"), T("psA")], writes=[u["qtr"]])

        prev_u = None
        for (c, p, blk) in units:
            u = unit_setup(c, p, blk)
            nk = len(u["kl"])
            pts = [emit_S(u, ii) for ii in range(min(2, nk))]
            for idx in range(nk):
                if idx + 2 < nk:
                    pts.append(emit_S(u, idx + 2))
                emit_PV(u, idx, pts[idx])
                if idx == min(1, nk - 1) and prev_u is not None:
                    emit_norm(prev_u)
            prev_u = u
        if prev_u is not None:
            emit_norm(prev_u)

    gg_state = {"i": 0}

    def load_gg(l, which, bsel):
        k = gg_state["i"] % 2
        gg_state["i"] += 1
        P.dma("sp", GG[:, k, :], gsc_d[l, which, bsel, :].partition_broadcast(128), s_gg[k],
              reads=[T("gsc")], writes=[T("GG", k)] + [T("hT2", kk) for kk in range(4)])
        return k

    y_state = {"i": 0}

    def resid(t, py, pykeys, ggk):
        col = ev["i"] % 4
        ev["i"] += 1
        sT_ = T("stat", col)
        ptr = [T(*k) for k in pykeys]
        P.emit("act", ACT(tmpb[:, 0:D], py[:, 0:D], AF.Square, accum_out=stat[:, 0, col:col + 1]), reads=ptr, writes=[sT_, T("tmp")])
        P.emit("act", ACT(stat[:, 1, col:col + 1], stat[:, 0, col:col + 1], AF.Sqrt, scale=1.0 / D, bias=eps_ap), reads=[c2], writes=[sT_])
        P.emit("dve", RCP(stat[:, 2, col:col + 1], stat[:, 1, col:col + 1]), writes=[sT_])
        P.emit("dve", STT(tmp[:], py[:, 0:D], stat[:, 2, col:col + 1], GG[:, ggk, :], ALU.mult, ALU.mult),
               reads=ptr + [sT_, T("GG", ggk)], writes=[T("tmp")])
        P.emit("dve", TT(xs[:, t, :], xs[:, t, :], tmp[:], ALU.add), reads=[T("tmp")], writes=[T("x", t)])

    def y_banks():
        k = y_state["i"] % 2
        y_state["i"] += 1
        if k == 0:
            return psD, [("psD", 0), ("psD", 1)]
        return psC, [("psC", 0), ("psC", 1)]

    def out_proj(b, l, li):
        slots = [w_pop("out", l, 0, s) for s in range(4)]
        ntile = NT if li["with_ctx"] else 16
        gk_lat = load_gg(l, 0, b)
        gk_ctx = load_gg(l, 0, nb) if li["with_ctx"] else None
        for t in range(ntile):
            py, pykeys = y_banks()
            n = 4 if t >= 16 else t // 4
            for q4 in range(4):
                for kc in range(8):
                    rd = [T("wsl", slots[q4]), T("QA", kc, n)]
                    if kc < li["nq"]:
                        rd += [T("QA", kc, n, 0), T("QA", kc, n, 1)]
                    P.emit("pe", MM(py[:, q4 * 256:(q4 + 1) * 256], QA[:, kc, t * 128:(t + 1) * 128],
                                    wsl[:, slots[q4], kc, :], kc == 0, kc == 7),
                           reads=rd, writes=[T(*pykeys[q4 // 2])], signal=(kc == 7))
            resid(t, py, pykeys, gk_ctx if t >= 16 else gk_lat)
        w_release(4)

    gu_state = {"i": 0}

    def ffn(b, l, li):
        P.retire(MIX_KEYS)
        P.dma_group([("pool", w2[:, j, :], wfout_d[l, :, j, :]) for j in range(NJ)], s_w2,
                    writes=[T("w2", j) for j in range(NJ)])
        gk_lat = load_gg(l, 1, b)
        gk_ctx = load_gg(l, 1, nb) if li["with_ctx"] else None
        fblocks = [blk for blk in blocks_of() if not (blk[0] == 4 and not li["with_ctx"])]
        norm_block(b, l, 1, fblocks[0][3])
        for bi, blk in enumerate(fblocks):
            n, s0, N, tiles = blk
            ntl = len(tiles)
            ntiles_next = fblocks[bi + 1][3] if bi + 1 < len(fblocks) else []
            for j in range(NJ):
                slot = w_pop("ffn", l, n, j)
                k = gu_state["i"] % 2
                gu_state["i"] += 1
                (pg, pgk), (pu, puk) = (bankB if k == 0 else bankC)
                for (pa, pk, off) in ((pg, pgk, 0), (pu, puk, 128)):
                    for kc in range(8):
                        P.emit("pe", MM(pa[:, 0:N], wsl[:, slot, kc, off:off + 128], hT[:, kc, 0:N], kc == 0, kc == 7),
                               reads=[T("wsl", slot)] + [T("hT", kk) for kk in range(ntl)], writes=[T(*pk)], signal=(kc == 7))
                w_release()
                sg, sgk = (r_t1, "r_t1") if k == 0 else (r_t2, "r_t2")
                P.emit("act", ACT(sg[:, 0:N], pg[:, 0:N], AF.Silu), reads=[T(*pgk)], writes=[T(sgk)])
                P.emit("dve", TT(mT[:, j, 0:N], sg[:, 0:N], pu[:, 0:N], ALU.mult), reads=[T(sgk), T(*puk)], writes=[T("mT", j)])
            for k, t in enumerate(tiles):
                xi = norm_pre(b, l, 1, ntiles_next[k]) if k < len(ntiles_next) else None
                py, pykeys = y_banks()
                for f in range(2):
                    for j in range(NJ):
                        P.emit("pe", MM(py[:, f * 512:(f + 1) * 512], mT[:, j, k * 128:(k + 1) * 128],
                                        w2[:, j, f * 512:(f + 1) * 512], j == 0, j == NJ - 1),
                               reads=[T("mT", j), T("w2", j)], writes=[T(*pykeys[f])], signal=(j == NJ - 1))
                resid(t, py, pykeys, gk_ctx if t >= 16 else gk_lat)
                if xi is not None:
                    norm_post(b, l, 1, ntiles_next[k], k, xi)
        P.retire(FFN_KEYS)

    def mixer(b, l, li):
        even = li["even"]
        va = VAe if even else VA
        vt = [T("VA", t) for t in range(NT)]
        for cpos in ((64,) if even else (64, 256)):
            P.emit("dve", MSET(va[:, :, cpos:cpos + 64], 0.0), writes=vt)
            P.emit("dve", MSET(va[:, :, cpos:cpos + 1], 1.0), writes=vt)
        mblocks = blocks_of()
        norm_block(b, l, 0, mblocks[0][3], HB[0])
        for bi, blk in enumerate(mblocks):
            hcur["b"] = HB[bi % 2]
            hcur["extras"] = (norm_thunks(b, l, 0, mblocks[bi + 1][3], HB[(bi + 1) % 2])
                              if bi + 1 < len(mblocks) else None)
            in_proj_block(b, l, li, blk)
            while hcur["extras"]:
                hcur["extras"].popleft()()
        hcur["b"] = HB[0]
        hcur["extras"] = None
        if even:
            pool_mixer(b, l, li)
        if 'noattn' not in DBG:
            attention(b, l, li)
        if 'noout' not in DBG:
            out_proj(b, l, li)
        else:
            [w_pop('out', l, 0, s_) for s_ in range(4)]
            w_release(4)

    eps_t = sb("eps_t", [128, 1], F32)
    eps_ap = eps_t[:, 0:1]
    P.emit("dve", MSET(eps_t[:], EPS), writes=[c2])

    for b in range(nb):
        for t in range(NT):
            src = x_d[b, t * 128:(t + 1) * 128, :] if t < 16 else ctx_d[b, (t - 16) * 128:(t - 15) * 128, :]
            P.dma("sp", xs[:, t, :], src, s_x[t], writes=[T("x", t)])
        for l in layers:
            li = layer_info(l)
            mixer(b, l, li)
            if 'noffn' not in DBG:
                ffn(b, l, li)
        for t in range(16):
            P.dma("sp", out_d[b, t * 128:(t + 1) * 128, :], xs[:, t, :], s_out[t], reads=[T("x", t)])
    assert 'noffn' in DBG or (not sched and not inflight), (len(sched), len(inflight))

    P.play(final_waits=s_out)
    es.close()
    return nc, P.n_inst


def _rope_tables():
    quarter = 16
    freqs = (np.float32(10000.0) ** (-np.arange(quarter, dtype=np.float32) / np.float32(quarter))).astype(np.float32)
    t = np.arange(SEQ)
    rows = (t // 64).astype(np.float32)
    cols = (t % 64).astype(np.float32)
    cosT = np.zeros((128, SEQ), np.float32)
    sinT = np.zeros((128, SEQ), np.float32)
    for p in range(128):
        d = p % 64
        pos = rows if d < 32 else cols
        j = d % 16
        ang = (pos * freqs[j]).astype(np.float32)
        sign = -1.0 if (d % 32) < 16 else 1.0
        cosT[p] = np.cos(ang)
        sinT[p] = sign * np.sin(ang)
    return cosT, sinT


def _const_mats():
    cm = np.zeros((128, 5, 128), np.float32)
    cm[:, 0, :] = np.eye(128, dtype=np.float32)
    for m in range(128):
        d = m % 64
        partner = m + 16 if (d % 32) < 16 else m - 16
        cm[partner, 1, m] = 1.0
    for k in range(128):
        for m in range(128):
            if k // 64 == m // 64:
                cm[k, 2, m] = 1.0 / 64.0
    a = np.arange(128)[None, :]
    bb = np.arange(128)[:, None]
    cm[:, 3, :] = (a <= bb).astype(np.float32)
    cm[:, 4, :] = (a >= bb).astype(np.float32)
    S = 384
    mb = np.zeros((128, 4, 5, 128), np.float32)
    for gi, w in enumerate(WINS):
        Mm = np.zeros((S, S), np.float64)
        for t in range(S):
            lo = max(t - w // 2, 0)
            hi = min(t + w - w // 2, S)
            Mm[lo:hi, t] = 1.0 / (hi - lo)
            Mm[t, t] -= 1.0
        mb[:, gi, 0, :] = Mm[0:128, 0:128]
        mb[:, gi, 1, :] = Mm[128:256, 128:256]
        mb[:, gi, 2, :] = Mm[256:384, 256:384]
        mb[:, gi, 3, :] = Mm[0:128, 128:256]
        mb[:, gi, 4, :] = Mm[256:384, 128:256]
    return cm, mb


def _slots(W, ncol_slots):
    return np.ascontiguousarray(W.reshape(8, 128, ncol_slots, 256).transpose(2, 1, 0, 3))


def prepare_shared(inp):
    f = lambda a: np.asarray(a, dtype=np.float32)
    w_mod, b_mod = f(inp["w_mod"]), f(inp["b_mod"])
    sh = {}
    sh["wmod"] = np.ascontiguousarray(w_mod.reshape(DEPTH, 8, 128, 12, 512).transpose(0, 3, 2, 1, 4))
    sh["bmodT"] = np.ascontiguousarray(b_mod.reshape(DEPTH, 48, 128).transpose(2, 0, 1))
    gpre = np.stack([f(inp["g_pre_mix"]), f(inp["g_pre_ffn"])], axis=1)
    sh["gpreT"] = np.ascontiguousarray(gpre.reshape(DEPTH, 2, 8, 128).transpose(3, 0, 1, 2))
    sh["_gpost"] = np.stack([f(inp["g_post_mix"]), f(inp["g_post_ffn"])], axis=1)
    sh["_bgate"] = np.stack([b_mod[:, 2048:3072], b_mod[:, 5120:6144]], axis=1)
    we_in, we_out = f(inp["we_in"]), f(inp["we_out"])
    cols = []
    for c in range(4):
        cols += list(range(c * 64, (c + 1) * 64)) + list(range((c + 4) * 64, (c + 5) * 64))
    qperm_e = np.array(cols)
    colperm = np.concatenate([qperm_e, np.arange(512, 1280)])
    sh["wein"] = np.stack([_slots(we_in[i][:, colperm], 5) for i in range(2)])
    rowperm = np.concatenate([qperm_e, np.arange(512, 1024)])
    sh["weout"] = np.stack([_slots(we_out[i][rowperm, :], 4) for i in range(2)])
    sh["wepool"] = np.ascontiguousarray(f(inp["we_pool"]).transpose(2, 0, 1, 3))
    qg, kg = f(inp["we_q_gain"]), f(inp["we_k_gain"])
    sh["wegain"] = np.ascontiguousarray(np.stack([np.tile(qg, (1, 2)), np.tile(kg, (1, 2))], axis=2).transpose(1, 0, 2))
    sh["wepsc"] = np.ascontiguousarray(f(inp["we_pool_scale"]).reshape(2, 4, 128).transpose(2, 0, 1))
    wo_in, wo_out = f(inp["wo_in"]), f(inp["wo_out"])
    cols = []
    for c in range(8):
        k, r = c // 4, c % 4
        ha, hb = 8 * k + r, 8 * k + r + 4
        cols += list(range(ha * 64, (ha + 1) * 64)) + list(range(hb * 64, (hb + 1) * 64))
    qperm_o = np.array(cols)
    colperm = np.concatenate([qperm_o, np.arange(1024, 1536)])
    sh["woin"] = np.stack([_slots(wo_in[i][:, colperm], 6) for i in range(2)])
    sh["woout"] = np.stack([_slots(wo_out[i][qperm_o, :], 4) for i in range(2)])
    sh["wosink"] = np.ascontiguousarray(np.broadcast_to(f(inp["wo_sink"])[None], (128, 2, 16)))
    wfi, wfo = f(inp["w_ffn_in"]), f(inp["w_ffn_out"])
    g = wfi[:, :, :HID].reshape(DEPTH, 8, 128, NJ, 128)
    u = wfi[:, :, HID:].reshape(DEPTH, 8, 128, NJ, 128)
    sh["wfin"] = np.ascontiguousarray(np.concatenate([g, u], axis=4).transpose(0, 3, 2, 1, 4))
    sh["wfout"] = np.ascontiguousarray(wfo.reshape(DEPTH, NJ, 128, D).transpose(0, 2, 1, 3))
    cosT, sinT = _rope_tables()
    sh["cosT"], sh["sinT"] = cosT, sinT
    sh["cmat"], sh["mband"] = _const_mats()
    return sh


def core_inputs(sh, inp, b0, nb):
    f = lambda a: np.asarray(a, dtype=np.float32)
    NB1 = nb + 1
    m = {k: v for k, v in sh.items() if not k.startswith("_")}
    m["x"] = np.ascontiguousarray(f(inp["x"])[b0:b0 + nb])
    m["ctx"] = np.ascontiguousarray(f(inp["ctx"])[b0:b0 + nb])
    call = np.concatenate([f(inp["c"])[b0:b0 + nb], f(inp["c_ctx"])[None, :]], axis=0)
    m["scT"] = np.ascontiguousarray(call.reshape(NB1, 8, 128).transpose(2, 1, 0))
    m["bgate"] = np.ascontiguousarray(np.broadcast_to(sh["_bgate"][None], (NB1, DEPTH, 2, D)))
    m["gpost"] = np.ascontiguousarray(np.broadcast_to(sh["_gpost"][None], (NB1, DEPTH, 2, D)))
    return m


_CACHE = {}


def kernel(**inputs):
    nb = 32 // N_CORES
    key = ("full", nb)
    if key not in _CACHE:
        _CACHE[key] = build_program(nb, list(range(DEPTH)))[0]
    nc = _CACHE[key]
    sh = prepare_shared(inputs)
    in_maps = [core_inputs(sh, inputs, c * nb, nb) for c in range(N_CORES)]
    res = run_bass_kernel_spmd(nc, in_maps, core_ids=list(range(N_CORES)))
    return np.concatenate([np.asarray(r["out"], dtype=np.float32) for r in res.results], axis=0)
```
